# Optimizing a Trainium2 kernel written in Bass

```python
import jax, jax.numpy as jnp
from jax import lax
import numpy as np

D_MODEL = 1024
BATCH = 32
SEQ = 2048
DEPTH = 2
DEC_BATCH = 4
DEC_SEQ = 8192
PAST_LEN = 128

GRID_W = 64
WIN_R = 8
WIN_C = 16
NA_HEADS = 8
NA_HEAD_DIM = 64
NA_WIDTH = NA_HEADS * NA_HEAD_DIM
FT_GROUPS = 4
FT_GROUP_DIM = 64
FT_WIDTH = FT_GROUPS * FT_GROUP_DIM
CV_WIDTH = 256
CV_KERNEL = 31
N_BRANCH = 3
NORM_EPS = 1e-6
NEG_INF = -1e30
SPLIT_SIZES = (NA_WIDTH, NA_WIDTH, NA_WIDTH, NA_WIDTH,
               FT_WIDTH, FT_WIDTH,
               CV_WIDTH, CV_WIDTH, CV_WIDTH,
               D_MODEL, D_MODEL, D_MODEL)
D_IN = sum(SPLIT_SIZES)

kernel_name = "hybrid_natten_fnet_conformer_encoder"


def rms_norm(x, g):
    x32 = x.astype(jnp.float32)
    y = x32 * lax.rsqrt(jnp.mean(x32 * x32, axis=-1, keepdims=True) + NORM_EPS)
    return (y * g.astype(jnp.float32)).astype(x.dtype)


def layer_norm(x, g, b):
    x32 = x.astype(jnp.float32)
    mu = jnp.mean(x32, axis=-1, keepdims=True)
    xc = x32 - mu
    var = jnp.mean(xc * xc, axis=-1, keepdims=True)
    y = xc * lax.rsqrt(var + NORM_EPS) * g.astype(jnp.float32) + b.astype(jnp.float32)
    return y.astype(x.dtype)


def neighbourhood_attention(q, k, v, rel_pos_bias):
    b, l = q.shape[0], q.shape[1]
    rows = l // GRID_W
    wr = min(WIN_R, rows)
    qg = q.reshape(b, rows, GRID_W, NA_HEADS, NA_HEAD_DIM) * (NA_HEAD_DIM ** -0.5)
    kg = k.reshape(b, rows, GRID_W, NA_HEADS, NA_HEAD_DIM)
    vg = v.reshape(b, rows, GRID_W, NA_HEADS, NA_HEAD_DIM)
    col = jnp.arange(GRID_W)
    c_start = jnp.clip(col - WIN_C // 2, 0, GRID_W - WIN_C)
    col_valid = (col[None, :] >= c_start[:, None]) & (col[None, :] < c_start[:, None] + WIN_C)
    dc_idx = jnp.clip(col[None, :] - col[:, None] + WIN_C - 1, 0, 2 * WIN_C - 2)
    bias_c = rel_pos_bias[:, :, dc_idx]

    def row_block(r):
        r_start = jnp.clip(r - wr // 2, 0, rows - wr)
        k_blk = lax.dynamic_slice_in_dim(kg, r_start, wr, axis=1)
        v_blk = lax.dynamic_slice_in_dim(vg, r_start, wr, axis=1)
        q_row = lax.dynamic_index_in_dim(qg, r, axis=1, keepdims=False)
        dr_idx = r_start + jnp.arange(wr) - r + WIN_R - 1
        bias = jnp.take(bias_c, dr_idx, axis=1).transpose(0, 2, 1, 3)
        s = jnp.einsum('bqhd,bikhd->bhqik', q_row, k_blk).astype(jnp.float32)
        s = s + bias[None].astype(jnp.float32)
        s = jnp.where(col_valid[None, None, :, None, :], s, NEG_INF)
        p = jax.nn.softmax(s.reshape(b, NA_HEADS, GRID_W, wr * GRID_W), axis=-1)
        p = p.reshape(b, NA_HEADS, GRID_W, wr, GRID_W).astype(v.dtype)
        return jnp.einsum('bhqik,bikhd->bqhd', p, v_blk)

    out = lax.map(row_block, jnp.arange(rows))
    return out.transpose(1, 0, 2, 3, 4).reshape(b, l, NA_WIDTH)


def fourier_mix(u):
    b, l, _ = u.shape
    ug = u.reshape(b, l, FT_GROUPS, FT_GROUP_DIM).astype(jnp.float32)
    f = jnp.fft.fft2(ug, axes=(1, 3), norm="ortho").real
    return f.reshape(b, l, FT_WIDTH).astype(u.dtype)


def conformer_conv(u_val, u_glu_gate, dw_w, dw_b, ln_g, ln_b):
    h = u_val * jax.nn.sigmoid(u_glu_gate)
    h = lax.conv_general_dilated(
        h, dw_w.astype(h.dtype)[:, None, :], window_strides=(1,),
        padding=[(CV_KERNEL // 2, CV_KERNEL // 2)],
        dimension_numbers=('NWC', 'WIO', 'NWC'), feature_group_count=CV_WIDTH) + dw_b
    return jax.nn.silu(layer_norm(h, ln_g, ln_b))


def encoder_layer(x, pre_g, post_g, w_in, rpb, dw_w, dw_b, cn_g, cn_b, w_a, w_b, w_c, w_o):
    b, l, _ = x.shape
    h = rms_norm(x, pre_g)
    z = h @ w_in
    split_pts = np.cumsum(SPLIT_SIZES)[:-1].tolist()
    (q, k, v, gate_a, u_b, gate_b, c_val, c_glu, gate_c,
     merge_a, merge_b, merge_c) = jnp.split(z, split_pts, axis=-1)
    heads = lambda t: t.reshape(b, l, NA_HEADS, NA_HEAD_DIM)
    y_a = neighbourhood_attention(heads(q), heads(k), heads(v), rpb) * jax.nn.silu(gate_a)
    y_b = fourier_mix(u_b) * jax.nn.silu(gate_b)
    y_c = conformer_conv(c_val, c_glu, dw_w, dw_b, cn_g, cn_b) * jax.nn.silu(gate_c)
    merged = (jax.nn.sigmoid(merge_a) * (y_a @ w_a)
              + jax.nn.sigmoid(merge_b) * (y_b @ w_b)
              + jax.nn.sigmoid(merge_c) * (y_c @ w_c))
    return x + rms_norm(merged @ w_o, post_g)


def setup_inputs(seed: int = 0) -> dict:
    key = jax.random.key(seed)
    ks = jax.random.split(key, 16)
    f32 = jnp.float32
    nrm = lambda k, shape, scale: (jax.random.normal(k, shape, f32) * scale).astype(f32)
    return {
        "x_prompt": nrm(ks[0], (BATCH, SEQ, D_MODEL), 1.0),
        "x_sample": nrm(ks[1], (DEC_BATCH, DEC_SEQ, D_MODEL), 1.0),
        "pre_norm_g": 1.0 + nrm(ks[2], (DEPTH, D_MODEL), 0.05),
        "post_norm_g": 1.0 + nrm(ks[3], (DEPTH, D_MODEL), 0.05),
        "w_in": nrm(ks[4], (DEPTH, D_MODEL, D_IN), D_MODEL ** -0.5),
        "rel_pos_bias": nrm(ks[5], (DEPTH, NA_HEADS, 2 * WIN_R - 1, 2 * WIN_C - 1), 0.5),
        "c_dw_w": nrm(ks[6], (DEPTH, CV_KERNEL, CV_WIDTH), CV_KERNEL ** -0.5),
        "c_dw_b": nrm(ks[7], (DEPTH, CV_WIDTH), 0.02),
        "c_norm_g": 1.0 + nrm(ks[8], (DEPTH, CV_WIDTH), 0.05),
        "c_norm_b": nrm(ks[9], (DEPTH, CV_WIDTH), 0.02),
        "w_a_out": nrm(ks[10], (DEPTH, NA_WIDTH, D_MODEL), NA_WIDTH ** -0.5),
        "w_b_out": nrm(ks[11], (DEPTH, FT_WIDTH, D_MODEL), FT_WIDTH ** -0.5),
        "w_c_out": nrm(ks[12], (DEPTH, CV_WIDTH, D_MODEL), CV_WIDTH ** -0.5),
        "w_o": nrm(ks[13], (DEPTH, D_MODEL, D_MODEL), D_MODEL ** -0.5),
    }


def reference(x_prompt, x_sample, pre_norm_g, post_norm_g, w_in, rel_pos_bias, c_dw_w, c_dw_b,
              c_norm_g, c_norm_b, w_a_out, w_b_out, w_c_out, w_o):
    y_prompt = x_prompt
    y_sample = x_sample
    for i in range(DEPTH):
        params = (pre_norm_g[i], post_norm_g[i], w_in[i], rel_pos_bias[i], c_dw_w[i], c_dw_b[i],
                  c_norm_g[i], c_norm_b[i], w_a_out[i], w_b_out[i], w_c_out[i], w_o[i])
        y_prompt = encoder_layer(y_prompt, *params)
        y_sample = encoder_layer(y_sample, *params)
    return (y_prompt, y_sample)
```

```python
import os
import numpy as np
from contextlib import ExitStack
import concourse.bass as bass
import concourse.mybir as mybir
from concourse.bass_utils import run_bass_kernel_spmd

F32 = mybir.dt.float32
BF16 = mybir.dt.bfloat16
ALU = mybir.AluOpType
AF = mybir.ActivationFunctionType

EPOCH = 30000
NDMA_SEM = 20
D = 1024
DIN = 6400
EPS = 1e-6
NEG = -30000.0
DEPTH = 2


class Prog:
    ENGS = ("pe", "act", "dve", "pool", "sp")

    def __init__(self, nc, stack):
        self.nc = nc
        self.stack = stack
        self.ops = {e: [] for e in self.ENGS}
        self.cnt = {e: 0 for e in self.ENGS}
        self.sems = {e: [] for e in self.ENGS}
        self.seen = {e: {} for e in self.ENGS}
        self.regions = {}
        self.dma_sems = {}
        self.dma_val = {}
        self.dma_rr = {"sp": 0, "pool": 0, "act": 0}
        self.nsem = {"sp": NDMA_SEM, "pool": 6}
        for q in ("sp", "pool"):
            self.dma_sems[q] = [self._mksem(f"d{q}{i}") for i in range(self.nsem[q])]

    def _mksem(self, name):
        s = self.stack.enter_context(self.nc.semaphore(name))
        self.dma_val[name] = 0
        return (name, s)

    def _eng_sem(self, e, n):
        idx = (n - 1) // EPOCH
        while len(self.sems[e]) <= idx:
            nm = f"s{e}{len(self.sems[e])}"
            self.sems[e].append((nm, self.stack.enter_context(self.nc.semaphore(nm))))
        return self.sems[e][idx], n - idx * EPOCH

    def _wait(self, eng, ticket):
        if ticket is None:
            return
        if ticket[0] == "eng":
            _, f, n = ticket
            (nm, s), v = self._eng_sem(f, n)
        else:
            _, (nm, s), v = ticket
        if self.seen[eng].get(nm, 0) >= v:
            return
        self.seen[eng][nm] = v
        self.ops[eng].append(lambda e, s=s, v=v: e.wait_ge(s, v))

    def _deps(self, eng, reads, writes):
        for k in reads:
            r = self.regions.get(k)
            if r is not None:
                for t in r["w"].values():
                    self._wait(eng, t)
        for k in writes:
            r = self.regions.get(k)
            if r is not None:
                for t in r["w"].values():
                    self._wait(eng, t)
                for t in r["r"].values():
                    self._wait(eng, t)

    def _record(self, ticket, reads, writes):
        rk = ticket[1] if ticket[0] == "eng" else ticket[1][0]
        for k in writes:
            r = self.regions.setdefault(k, {"w": {}, "r": {}})
            r["w"][rk] = ticket
            r["r"] = {}
        for k in reads:
            if k in writes:
                continue
            r = self.regions.setdefault(k, {"w": {}, "r": {}})
            r["r"][rk] = ticket

    def op(self, eng, fn, reads=(), writes=()):
        return self.group(eng, [fn], reads, writes)

    def group(self, eng, fns, reads=(), writes=()):
        self._deps(eng, reads, writes)
        for fn in fns[:-1]:
            self.ops[eng].append(lambda e, fn=fn: fn(e))
        self.cnt[eng] += 1
        n = self.cnt[eng]
        (nm, s), v = self._eng_sem(eng, n)
        fn = fns[-1]
        self.ops[eng].append(lambda e, fn=fn, s=s: fn(e).then_inc(s, 1))
        t = ("eng", eng, n)
        self._record(t, reads, writes)
        return t

    def dma(self, q, fn, reads=(), writes=()):
        self._deps(q, reads, writes)
        i = self.dma_rr[q]
        self.dma_rr[q] = (i + 1) % self.nsem[q]
        nm, s = self.dma_sems[q][i]
        prev = self.dma_val[nm]
        if prev > 0:
            self._wait(q, ("dma", (nm, s), prev))
        v = prev + 16
        self.dma_val[nm] = v
        self.ops[q].append(lambda e, fn=fn, s=s: fn(e).then_inc(s, 16))
        t = ("dma", (nm, s), v)
        self._record(t, reads, writes)
        return t

    def barrier(self):
        for e in self.ENGS:
            for f in ("pe", "act", "dve", "pool"):
                if self.cnt[f] > 0:
                    self._wait(e, ("eng", f, self.cnt[f]))
            for q in ("sp", "pool"):
                for nm, s in self.dma_sems[q]:
                    v = self.dma_val[nm]
                    if v > 0:
                        self._wait(e, ("dma", (nm, s), v))
        self.regions = {}

    def finish(self):
        self.barrier()
        nc = self.nc
        with nc.Block() as blk:
            @blk.tensor
            def _(e):
                for f in self.ops["pe"]:
                    f(e)

            @blk.scalar
            def _(e):
                for f in self.ops["act"]:
                    f(e)

            @blk.vector
            def _(e):
                for f in self.ops["dve"]:
                    f(e)

            @blk.gpsimd
            def _(e):
                for f in self.ops["pool"]:
                    f(e)

            @blk.sync
            def _(e):
                for f in self.ops["sp"]:
                    f(e)


def MM(out, lhsT, rhs, start, stop):
    return lambda e: e.matmul(out, lhsT=lhsT, rhs=rhs, start=start, stop=stop)


def TR(out, in_, ident):
    return lambda e: e.transpose(out, in_, ident)


def ACT(out, in_, func, **kw):
    return lambda e: e.activation(out=out, in_=in_, func=func, **kw)


def TT(out, in0, in1, op):
    return lambda e: e.tensor_tensor(out=out, in0=in0, in1=in1, op=op)


def TS(out, in0, s1, op0):
    return lambda e: e.tensor_scalar(out=out, in0=in0, scalar1=s1, scalar2=None, op0=op0)


def STT(out, in0, scalar, in1, op0, op1):
    return lambda e: e.scalar_tensor_tensor(out=out, in0=in0, scalar=scalar, in1=in1, op0=op0, op1=op1)


def CP(out, in_):
    return lambda e: e.tensor_copy(out=out, in_=in_)


def RCP(out, in_):
    return lambda e: e.reciprocal(out=out, in_=in_)


def MS(ap, v):
    return lambda e: e.memset(ap, v)


def DMA(out, in_):
    return lambda e: e.dma_start(out=out, in_=in_)


def r_start(r, R):
    return min(max(r - 4, 0), R - 8)


def qblock_info(r0, R):
    s0 = r_start(r0, R) - r0
    s1 = r_start(r0 + 1, R) - r0
    lo = r_start(r0, R)
    hi = r_start(r0 + 1, R) + 8
    chunks = list(range(lo // 2, (hi + 1) // 2))
    return (s0, s1), [(c, 2 * c - r0) for c in chunks]


def unitA_qblock(r0, RA, RP):
    ts_, chs = qblock_info(r0, RA)
    base = (r0 // RP) * RP
    tp_, chp = qblock_info(r0 - base, RP)
    ds = {Dv for _, Dv in chs} | {Dv for _, Dv in chp}
    return (ts_, tp_), [((r0 + Dv) // 2, Dv) for Dv in sorted(ds)]


INT_T = (-4, -3)
INT_D = (-4, -2, 0, 2, 4)


def qb_desc(unit, r0, RA, RP):
    if unit == "A":
        key, ch = unitA_qblock(r0, RA, RP)
    else:
        t, ch = qblock_info(r0, RP)
        key = (t, t)
    interior = key == (INT_T, INT_T) and tuple(d for _, d in ch) == INT_D
    return key, ch, interior


def enum_rowmask(LA, NSUB, NB):
    idx = {}
    RA = LA // 64
    RP = RA // NSUB
    for r0 in range(0, RA, 2):
        key, ch, interior = qb_desc("A", r0, RA, RP)
        if not interior:
            for c, Dv in ch:
                idx.setdefault((key[0], key[1], Dv), len(idx))
    if NB:
        for r0 in range(0, RP, 2):
            key, ch, interior = qb_desc("B", r0, RA, RP)
            if not interior:
                for c, Dv in ch:
                    idx.setdefault((key[0], key[1], Dv), len(idx))
    return idx


def rowmask_block(typ, Dv):
    m = np.full((128, 128), NEG, np.float32)
    for krp in range(2):
        for qrp in range(2):
            s = typ[qrp]
            if s <= Dv + krp < s + 8:
                m[krp * 64:(krp + 1) * 64, qrp * 64:(qrp + 1) * 64] = 0.0
    return m


def make_rowmask(idx, mode):
    m = np.zeros((128, max(len(idx), 1), 128), np.float32)
    for (ts_, tp_, Dv), i in idx.items():
        m[:, i, :] = rowmask_block(ts_ if mode == "s" else tp_, Dv)
    return m


def make_rowmask_int():
    return np.stack([rowmask_block(INT_T, Dv) for Dv in INT_D], axis=1)


def make_colmask():
    col = np.arange(64)
    cs = np.clip(col - 8, 0, 48)
    colv = (col[:, None] >= cs[None, :]) & (col[:, None] < cs[None, :] + 16)
    m = np.full((128, 7, 128), NEG, np.float32)
    for d in range(7):
        Dv = 2 * d - 6
        for krp in range(2):
            for qrp in range(2):
                if abs(Dv + krp - qrp) <= 7:
                    m[krp * 64:(krp + 1) * 64, d, qrp * 64:(qrp + 1) * 64] = np.where(colv, 0.0, NEG)
    return m


def gather_bias(rpb):
    col = np.arange(64)
    dc = np.clip(col[:, None] - col[None, :] + 15, 0, 30)
    out = np.zeros((128, 7, 8, 128), np.float32)
    for d in range(7):
        Dv = 2 * d - 6
        for krp in range(2):
            for qrp in range(2):
                dr = int(np.clip(Dv + krp - qrp + 7, 0, 14))
                blk = rpb[:, dr][:, dc]
                out[krp * 64:(krp + 1) * 64, d, :, qrp * 64:(qrp + 1) * 64] = blk.transpose(1, 0, 2)
    return out


def dft_consts(L):
    A = L // 128
    s = 1.0 / np.sqrt(64.0 * L)
    a = np.arange(A)
    ang1 = 2 * np.pi * np.outer(a, a) / A
    w1c = (np.cos(ang1) * s).astype(np.float32)
    w1s = (-np.sin(ang1) * s).astype(np.float32)
    b = np.arange(128, dtype=np.float64)
    ka = np.arange(A, dtype=np.float64)
    kb = np.arange(128, dtype=np.float64)
    k = ka[:, None] + A * kb[None, :]
    ang2 = 2 * np.pi * b[:, None, None] * k[None] / L
    mr = np.cos(ang2)
    mi = -np.sin(ang2)
    m2 = np.zeros((128, A, 2, 256), np.float32)
    m2[:, :, 0, 0:128] = mr
    m2[:, :, 0, 128:256] = mi
    m2[:, :, 1, 0:128] = -mi
    m2[:, :, 1, 128:256] = mr
    return w1c, w1s, m2


def w1_blockdiag(G, L):
    A = L // 128
    w1c, w1s, _ = dft_consts(L)
    K = G * A
    out = np.zeros((K, 2, K), np.float32)
    for g in range(G):
        out[g * A:(g + 1) * A, 0, g * A:(g + 1) * A] = w1c
        out[g * A:(g + 1) * A, 1, g * A:(g + 1) * A] = w1s
    return out


def chan_consts():
    c = np.arange(64)
    ang = 2 * np.pi * np.outer(c, c) / 64
    C = np.zeros((128, 128), np.float32)
    S = np.zeros((128, 128), np.float32)
    for g in range(2):
        C[g * 64:(g + 1) * 64, g * 64:(g + 1) * 64] = np.cos(ang)
        S[g * 64:(g + 1) * 64, g * 64:(g + 1) * 64] = np.sin(ang)
    return C, S


def build(LA, NSUB, NB_, depth=DEPTH, stop=None):
    LP = LA // NSUB
    T = LA + NB_ * LP
    NT = T // 512
    RA = LA // 64
    RP = LP // 64
    NTA = LA // 512
    TPS = LP // 512
    bidx = enum_rowmask(LA, NSUB, NB_)
    NB = max(len(bidx), 1)
    Ls = sorted({LA, LP})
    groups = [(0, 1, LA, 0, 0), (0, NSUB, LP, 1, 1)]
    if NB_:
        groups.append((LA, NB_, LP, 0, None))
    gcfg = sorted({(G, L) for _, G, L, _, _ in groups})

    nc = bass.Bass("TRN2", target_bir_lowering=False)
    dram = lambda n, sh, dt, kind="Internal": nc.dram_tensor(n, sh, dt, kind=kind).ap()
    x_in = dram("x", [T, D], F32, "ExternalInput")
    y_out = dram("y", [T, D], F32, "ExternalOutput")
    w_in = dram("w_in", [depth, D, DIN], F32, "ExternalInput")
    w_abc = dram("w_abc", [depth, D, D], F32, "ExternalInput")
    w_o = dram("w_o", [depth, D, D], F32, "ExternalInput")
    pre_g = dram("pre_g", [depth, 128, 8], F32, "ExternalInput")
    post_g = dram("post_g", [depth, 128, D], F32, "ExternalInput")
    dww = dram("dww", [depth, 128, 2, 31], F32, "ExternalInput")
    cvec = dram("cvec", [depth, 128, 3, 2], F32, "ExternalInput")
    biasg = dram("biasg", [depth, 128, 7 * 8 * 128], F32, "ExternalInput")
    rmintd = dram("rmintd", [128, 5 * 128], F32, "ExternalInput")
    cmd = dram("cmd", [128, 7 * 128], F32, "ExternalInput")
    maskd = dram("maskd", [128, NB * 128], F32, "ExternalInput")
    identd = dram("identd", [128, 128], F32, "ExternalInput")
    c64d = dram("c64d", [128, 2, 128], F32, "ExternalInput")
    w1d = {(G, L): dram(f"w1d{G}_{L}", [2 * G * L // 128, 2, 2 * G * L // 128], F32, "ExternalInput") for (G, L) in gcfg}
    flagd = dram("flagd", [128, 2], F32, "ExternalInput")
    m2d = {L: dram(f"m2d{L}", [128, (L // 128) * 2, 256], F32, "ExternalInput") for L in Ls}
    xmid = dram("xmid", [T, D], F32)
    m2b = {L: dram(f"m2b{L}", [128, (L // 128) * 2, 256], BF16) for L in Ls}
    hTd = dram("hTd", [8, 128, T], BF16)
    kTd = dram("kTd", [4, 128, T], BF16)
    vd = dram("vd", [T, 520], BF16)
    Ud = dram("Ud", [T, 256], BF16)
    Zdg = [dram(f"Zd{i}", [2, G_ * L_ * 256], BF16) for i, (_, G_, L_, _, _) in enumerate(groups)]
    Ybd2 = [dram(f"Ybd{i}", [2, 128, T], BF16) for i in range(2)]
    hcd = dram("hcd", [2, 128, T], BF16)
    ycd = dram("ycd", [8, 128, T], BF16)

    with ExitStack() as st:
        P = Prog(nc, st)

        uid = [0]

        def SB(stk, n, sh, dt):
            uid[0] += 1
            return stk.enter_context(nc.sbuf_tensor(f"{n}_{uid[0]}", sh, dt))

        def PS(stk, n, sh, dt):
            uid[0] += 1
            return stk.enter_context(nc.psum_tensor(f"{n}_{uid[0]}", sh, dt))

        ident = SB(st, "ident", [128, 128], BF16)
        onesb = SB(st, "onesb", [128, 128], BF16)
        onesf = SB(st, "onesf", [128, 128], F32)
        c64 = SB(st, "c64", [128, 2, 128], BF16)
        P.dma("pool", DMA(ident[:], identd), writes=["ident"])
        P.dma("pool", DMA(c64[:], c64d), writes=["c64"])
        flg = SB(st, "flg", [128, 2], F32)
        P.dma("sp", DMA(flg[:], flagd), writes=["flg"])
        P.op("dve", MS(onesb[:], 1.0), writes=["onesb"])
        P.op("dve", MS(onesf[:], 1.0 / 256.0), writes=["onesf"])

        for l in range(depth):
            xin = x_in if l == 0 else xmid
            xout = y_out if l == depth - 1 else xmid
            if depth == 1:
                xout = y_out
            wv = w_in[l].rearrange("(j p) c -> p j c", p=128)
            P.barrier()
            with ExitStack() as s1:
                w1 = SB(s1, "w1", [128, 8, 1792], BF16)
                for (dst, src, n) in ((0, 512, 512), (512, 1024, 512), (1024, 2048, 256),
                                      (1280, 2560, 256), (1536, 2816, 256)):
                    P.dma("pool", DMA(w1[:, :, dst:dst + n], wv[:, :, src:src + n]), writes=[f"w1_{dst}"])
                W1K = ["w1_0", "w1_512", "w1_1024", "w1_1280", "w1_1536"]
                if l == 0:
                    cst = [SB(s1, f"cst{i}", [128, 8, 256], BF16) for i in range(2)]
                    kk = 0
                    for L_ in Ls:
                        for h2 in range(0, 2 * (L_ // 128), 8):
                            P.dma("pool", DMA(cst[kk % 2][:], m2d[L_][:, h2:h2 + 8, :]), writes=[f"cst{kk % 2}"])
                            P.dma("pool", DMA(m2b[L_][:, h2:h2 + 8, :], cst[kk % 2][:]), reads=[f"cst{kk % 2}"])
                            kk += 1
                g1 = SB(s1, "g1", [128, 8], F32)
                P.dma("sp", DMA(g1[:], pre_g[l]), writes=["g1"])
                xts = [SB(s1, f"xt{i}", [128, 4, D], F32) for i in range(2)]
                hb = SB(s1, "hb", [128, 4, D], BF16)
                junk = SB(s1, "junk", [128, D], F32)
                ss = SB(s1, "ss", [128, 4], F32)
                rs = SB(s1, "rs", [128, 4], F32)
                hTs = [SB(s1, f"hT{i}", [128, 8, 512], BF16) for i in range(2)]
                kTs = SB(s1, "kTs", [128, 4, 512], BF16)
                vts = SB(s1, "vts", [128, 4, 520], BF16)
                uts = SB(s1, "uts", [128, 4, 256], BF16)
                th = SB(s1, "th", [128, 2, 512], F32)
                hcs = SB(s1, "hcs", [128, 2, 512], BF16)
                tps = [PS(s1, f"tp{i}", [128, 1024], BF16) for i in range(2)]
                acc = [PS(s1, f"acc{i}", [128, 512], F32) for i in range(5)]
                na = [0]

                def nacc():
                    na[0] += 1
                    i = na[0] % 5
                    return acc[i], f"acc{i}"

                def load_x(g):
                    b = g % 2
                    P.dma("sp", DMA(xts[b][:], xin[g * 512:(g + 1) * 512, :].rearrange("(s p) d -> p s d", p=128)),
                          writes=[f"xt{b}"])

                P.op("pool", MS(vts[:], 1.0), writes=["vts"])
                hbs = [hb, SB(s1, "hb2", [128, 4, D], BF16)]

                def norm(g):
                    b = g % 2
                    xt = xts[b]
                    hbb = hbs[b]
                    P.op("dve", MS(ss[:], 0.0), writes=["ss"])
                    for s in range(4):
                        P.op("act", ACT(junk[:], xt[:, s, :], AF.Square, accum_out=ss[:, s:s + 1]),
                             reads=[f"xt{b}"], writes=["junk", "ss"])
                    P.op("act", ACT(rs[:], ss[:], AF.Sqrt, bias=EPS, scale=1.0 / D), reads=["ss"], writes=["rs"])
                    P.op("dve", RCP(rs[:], rs[:]), reads=["rs"], writes=["rs"])
                    for s in range(4):
                        if s % 2 == 0:
                            P.op("act", ACT(hbb[:, s, :], xt[:, s, :], AF.Copy, scale=rs[:, s:s + 1]),
                                 reads=[f"xt{b}", "rs"], writes=[f"hb{b}_{s}"])
                        else:
                            P.op("dve", TS(hbb[:, s, :], xt[:, s, :], rs[:, s:s + 1], ALU.mult),
                                 reads=[f"xt{b}", "rs"], writes=[f"hb{b}_{s}"])

                load_x(0)
                if NT > 1:
                    load_x(1)
                norm(0)
                for g in range(NT):
                    b = g % 2
                    t0 = g * 512
                    hT = hTs[b]
                    hbb = hbs[b]
                    for j in range(8):
                        tp = tps[j % 2]
                        P.group("pe", [TR(tp[:, s * 128:(s + 1) * 128], hbb[:, s, j * 128:(j + 1) * 128], ident[:])
                                       for s in range(4)],
                                reads=[f"hb{b}_{s}" for s in range(4)] + ["ident"], writes=[f"tp{j % 2}"])
                        if j % 2 == 0:
                            P.op("dve", TS(hT[:, j, :], tp[:, 0:512], g1[:, j:j + 1], ALU.mult),
                                 reads=[f"tp{j % 2}", "g1"], writes=[f"hT{b}"])
                        else:
                            P.op("act", ACT(hT[:, j, :], tp[:, 0:512], AF.Copy, scale=g1[:, j:j + 1]),
                                 reads=[f"tp{j % 2}", "g1"], writes=[f"hT{b}"])
                    if g + 1 < NT:
                        norm(g + 1)
                    if g + 2 < NT:
                        load_x(g + 2)
                    P.dma("sp", DMA(hTd[:, :, t0:t0 + 512].rearrange("j p n -> p j n"), hT[:]), reads=[f"hT{b}"])
                    for m in range(4):
                        a, an = nacc()
                        P.group("pe", [MM(a[:], w1[:, j, m * 128:(m + 1) * 128], hT[:, j, :], j == 0, j == 7)
                                       for j in range(8)], reads=["w1_0", f"hT{b}"], writes=[an])
                        if m % 2 == 0:
                            P.op("act", ACT(kTs[:, m, :], a[:], AF.Copy), reads=[an], writes=["kTs"])
                        else:
                            P.op("dve", CP(kTs[:, m, :], a[:]), reads=[an], writes=["kTs"])
                    P.dma("sp", DMA(kTd[:, :, t0:t0 + 512].rearrange("j p n -> p j n"), kTs[:]), reads=["kTs"])
                    for s in range(4):
                        a, an = nacc()
                        P.group("pe", [MM(a[:], hT[:, j, s * 128:(s + 1) * 128], w1[:, j, 512:1024], j == 0, j == 7)
                                       for j in range(8)], reads=["w1_512", f"hT{b}"], writes=[an])
                        vdst = vts[:, s, :].rearrange("p (h e) -> p h e", e=65)[:, :, 0:64]
                        vsrc = a[:].rearrange("p (h e) -> p h e", e=64)
                        if s % 2 == 0:
                            P.op("dve", CP(vdst, vsrc), reads=[an], writes=["vts"])
                        else:
                            P.op("act", ACT(vdst, vsrc, AF.Copy), reads=[an], writes=["vts"])
                    P.dma("sp", DMA(vd[t0:t0 + 512, :].rearrange("(s p) d -> p s d", p=128), vts[:]), reads=["vts"])
                    for s in range(4):
                        a, an = nacc()
                        P.group("pe", [MM(a[:, 0:256], hT[:, j, s * 128:(s + 1) * 128], w1[:, j, 1024:1280], j == 0, j == 7)
                                       for j in range(8)], reads=["w1_1024", f"hT{b}"], writes=[an])
                        P.op("dve", CP(uts[:, s, :], a[:, 0:256]), reads=[an], writes=["uts"])
                    P.dma("sp", DMA(Ud[t0:t0 + 512, :].rearrange("(s p) d -> p s d", p=128), uts[:]), reads=["uts"])
                    for ch in range(2):
                        a, an = nacc()
                        P.group("pe", [MM(a[:], w1[:, j, 1536 + ch * 128:1536 + (ch + 1) * 128], hT[:, j, :], j == 0, j == 7)
                                       for j in range(8)], reads=["w1_1536", f"hT{b}"], writes=[an])
                        P.op("act", ACT(th[:, ch, :], a[:], AF.Tanh, scale=0.5), reads=[an], writes=[f"th{ch}"])
                        a2, an2 = nacc()
                        P.group("pe", [MM(a2[:], w1[:, j, 1280 + ch * 128:1280 + (ch + 1) * 128], hT[:, j, :], j == 0, j == 7)
                                       for j in range(8)], reads=["w1_1280", f"hT{b}"], writes=[an2])
                        P.op("dve", STT(hcs[:, ch, :], th[:, ch, :], 1.0, a2[:], ALU.add, ALU.mult),
                             reads=[an2, f"th{ch}"], writes=["hcs"])
                    P.dma("sp", DMA(hcd[:, :, t0:t0 + 512].rearrange("c p n -> p c n"), hcs[:]), reads=["hcs"])
            P.barrier()
            if stop == "p1":
                break
            KMAX = max(G * L // 128 for _, G, L, _, _ in groups)
            TGMAX = max(G * L for _, G, L, _, _ in groups)
            with ExitStack() as s2:
                usb = SB(s2, "usb", [2 * KMAX, 16384], BF16)
                zst = [SB(s2, f"zst{i}", [2 * KMAX, 2, 4096], BF16) for i in range(2)]
                zp = [PS(s2, f"zp{i}", [128, 512], F32) for i in range(4)]
                for gi_, (ts0, G, L, obuf, fcol) in enumerate(groups):
                    A = L // 128
                    K = G * A
                    K2 = 2 * K
                    TG = G * L
                    w1c = SB(s2, f"w1c{gi_}", [K2, 2, K2], BF16)
                    P.dma("pool", DMA(w1c[:], w1d[(G, L)]), writes=[f"w1c{gi_}"])
                    udv = Ud[ts0:ts0 + TG, :].rearrange("(a b) c -> a (b c)", b=128)
                    for gq in range(4):
                        for hb in range(2):
                            P.dma("sp", DMA(usb[hb * K:(hb + 1) * K, gq * 4096:(gq + 1) * 4096],
                                            udv[:, hb * 16384 + gq * 4096:hb * 16384 + (gq + 1) * 4096]), writes=[f"usb{gq}_{hb}"])
                    for gq in range(4):
                        zs = zst[gq % 2]
                        for n8 in range(8):
                            n = gq * 8 + n8
                            for r in range(2):
                                pi = (n8 % 2) * 2 + r
                                P.group("pe", [MM(zp[pi][0:K2, :], w1c[:, r, :], usb[0:K2, n * 512:(n + 1) * 512], True, True)],
                                        reads=[f"w1c{gi_}", f"usb{gq}_0", f"usb{gq}_1"], writes=[f"zp{pi}"])
                                if r == 0:
                                    P.op("act", ACT(zs[0:K2, r, n8 * 512:(n8 + 1) * 512], zp[pi][0:K2, :], AF.Copy),
                                         reads=[f"zp{pi}"], writes=[f"zst{gq % 2}"])
                                else:
                                    P.op("dve", CP(zs[0:K2, r, n8 * 512:(n8 + 1) * 512], zp[pi][0:K2, :]),
                                         reads=[f"zp{pi}"], writes=[f"zst{gq % 2}"])
                        for r in range(2):
                            zdv = Zdg[gi_][r, :].rearrange("(a n) -> a n", a=K)
                            for hb in range(2):
                                P.dma("sp", DMA(zdv[:, hb * 16384 + gq * 4096:hb * 16384 + (gq + 1) * 4096], zs[hb * K:(hb + 1) * K, r, :]),
                                      reads=[f"zst{gq % 2}"], writes=[f"Zd{gi_}_{r}_{hb}"])
            P.barrier()
            with ExitStack() as s2:
                AS = LA // 128
                AP_ = LP // 128
                m2s = SB(s2, "m2s", [128, AS * 2, 256], BF16)
                m2p = SB(s2, "m2p", [128, AP_ * 2, 256], BF16)
                zb = [SB(s2, f"zb{r}", [128, KMAX, 256], BF16) for r in range(2)]
                xr = SB(s2, "xr", [128, TGMAX], BF16)
                xi = SB(s2, "xi", [128, TGMAX], BF16)
                yst = SB(s2, "yst", [128, TGMAX], BF16)
                xp = [PS(s2, f"xp{i}", [128, 512], F32) for i in range(3)]
                yp = [PS(s2, f"yp{i}", [128, 512], F32) for i in range(2)]
                for (mt, Lm, nm) in ((m2s, LA, "m2s"), (m2p, LP, "m2p")):
                    for h2 in range(0, 2 * (Lm // 128), 32):
                        h3 = min(2 * (Lm // 128), h2 + 32)
                        P.dma("sp", DMA(mt[:, h2:h3, :], m2b[Lm][:, h2:h3, :]), writes=[f"{nm}_{h2}"])
                for gi_, (ts0, G, L, obuf, fcol) in enumerate(groups):
                    A = L // 128
                    K = G * A
                    TG = G * L
                    m2, m2n = (m2s, "m2s") if (L == LA and gi_ == 0) else (m2p, "m2p")
                    for a0_ in range(0, K, 16):
                        a1_ = min(K, a0_ + 16)
                        for r in range(2):
                            zv = Zdg[gi_][r, :].rearrange("(a b c) -> b a c", a=K, b=128, c=256)
                            P.dma("sp", DMA(zb[r][:, a0_:a1_, :], zv[:, a0_:a1_, :]), writes=[f"zb{r}_{a0_}"])
                    for cc in range(2):
                        for idx_ in range(K):
                            ka = idx_ % A
                            xpp = xp[idx_ % 3]
                            P.group("pe", [MM(xpp[:, 0:256], zb[0][:, idx_, cc * 128:(cc + 1) * 128], m2[:, ka * 2 + 0, :], True, False),
                                           MM(xpp[:, 0:256], zb[1][:, idx_, cc * 128:(cc + 1) * 128], m2[:, ka * 2 + 1, :], False, True)],
                                    reads=[f"zb0_{(idx_ // 16) * 16}", f"zb1_{(idx_ // 16) * 16}", f"{m2n}_{((ka * 2) // 32) * 32}"],
                                    writes=[f"xp{idx_ % 3}"])
                            xrv = xr[:, idx_ * 128:(idx_ + 1) * 128]
                            xiv = xi[:, idx_ * 128:(idx_ + 1) * 128]
                            if idx_ % 2 == 0:
                                P.op("act", ACT(xrv, xpp[:, 0:128], AF.Copy), reads=[f"xp{idx_ % 3}"], writes=["xr"])
                                P.op("act", ACT(xiv, xpp[:, 128:256], AF.Copy), reads=[f"xp{idx_ % 3}"], writes=["xi"])
                            else:
                                P.op("dve", CP(xrv, xpp[:, 0:128]), reads=[f"xp{idx_ % 3}"], writes=["xr"])
                                P.op("dve", CP(xiv, xpp[:, 128:256]), reads=[f"xp{idx_ % 3}"], writes=["xi"])
                        nkb = 512 // A
                        cnt_ = 0
                        for g_ in range(G):
                            for n in range(L // 512):
                                ypp = yp[cnt_ % 2]
                                ypn = f"yp{cnt_ % 2}"
                                xrs = xr[:, g_ * L:(g_ + 1) * L].rearrange("p (a kb) -> p kb a", a=A)[:, n * nkb:(n + 1) * nkb, :]
                                xis = xi[:, g_ * L:(g_ + 1) * L].rearrange("p (a kb) -> p kb a", a=A)[:, n * nkb:(n + 1) * nkb, :]
                                P.group("pe", [MM(ypp[:].rearrange("p (kb a) -> p kb a", a=A), c64[:, 0, :], xrs, True, False),
                                               MM(ypp[:].rearrange("p (kb a) -> p kb a", a=A), c64[:, 1, :], xis, False, True)],
                                        reads=["c64", "xr", "xi"], writes=[ypn])
                                yo = yst[:, g_ * L + n * 512:g_ * L + (n + 1) * 512]
                                if fcol is None:
                                    if cnt_ % 2 == 0:
                                        P.op("act", ACT(yo, ypp[:], AF.Copy), reads=[ypn], writes=["yst"])
                                    else:
                                        P.op("dve", CP(yo, ypp[:]), reads=[ypn], writes=["yst"])
                                else:
                                    if cnt_ % 2 == 0:
                                        P.op("act", ACT(yo, ypp[:], AF.Copy, scale=flg[:, fcol:fcol + 1]), reads=[ypn, "flg"], writes=["yst"])
                                    else:
                                        P.op("dve", TS(yo, ypp[:], flg[:, fcol:fcol + 1], ALU.mult), reads=[ypn, "flg"], writes=["yst"])
                                cnt_ += 1
                        P.dma("sp", DMA(Ybd2[obuf][cc, :, ts0:ts0 + TG], yst[:, 0:TG]), reads=["yst"])
            P.barrier()
            if stop == "fnet":
                break
            with ExitStack() as s3:
                w2 = SB(s3, "w2", [128, 8, 1536], BF16)
                for (dst, src, n) in ((0, 0, 512), (512, 1536, 512), (1024, 2304, 256), (1280, 3072, 256)):
                    P.dma("pool", DMA(w2[:, :, dst:dst + n], wv[:, :, src:src + n]), writes=[f"w2_{dst}"])
                W2K = ["w2_0", "w2_512", "w2_1024", "w2_1280"]
                bt = SB(s3, "bt", [128, 5, 8, 128], BF16)
                bD = SB(s3, "bD", [128, 7, 8, 128], BF16)
                rm = SB(s3, "rm", [128, NB, 128], BF16)
                with ExitStack() as s3b:
                    bg = SB(s3b, "bg", [128, 7, 8, 128], F32)
                    cm = SB(s3b, "cm", [128, 7, 128], F32)
                    rmi = SB(s3b, "rmi", [128, 5, 128], F32)
                    P.dma("sp", DMA(bg[:].rearrange("p a h q -> p (a h q)"), biasg[l]), writes=["bg"])
                    P.dma("sp", DMA(cm[:].rearrange("p n q -> p (n q)"), cmd), writes=["cm"])
                    P.dma("sp", DMA(rmi[:].rearrange("p n q -> p (n q)"), rmintd), writes=["rmi"])
                    P.dma("pool", DMA(rm[:].rearrange("p n q -> p (n q)"), maskd), writes=["rm"])
                    for d in range(7):
                        for h in range(8):
                            P.op("dve", TT(bg[:, d, h, :], bg[:, d, h, :], cm[:, d, :], ALU.add),
                                 reads=["bg", "cm"], writes=[f"bgd{d}"])
                        P.op("act", ACT(bD[:, d, :, :], bg[:, d, :, :], AF.Copy), reads=[f"bgd{d}"], writes=["bD"])
                    for di, Dv in enumerate(INT_D):
                        d = (Dv + 6) // 2
                        for h in range(8):
                            P.op("dve", TT(bt[:, di, h, :], bg[:, d, h, :], rmi[:, di, :], ALU.add),
                                 reads=[f"bgd{d}", "rmi"], writes=["bt"])
                    P.barrier()
                dwf = SB(s3, "dwf", [128, 2, 31], F32)
                cv = SB(s3, "cv", [128, 3, 2], F32)
                dg = SB(s3, "dg", [128, 2, 31, 128], BF16)
                identf = SB(s3, "identf", [128, 128], F32)
                P.dma("sp", DMA(dwf[:], dww[l]), writes=["dwf"])
                P.dma("sp", DMA(cv[:], cvec[l]), writes=["cv"])
                P.dma("sp", DMA(identf[:], identd), writes=["identf"])
                for ch in range(2):
                    for j in range(31):
                        if j % 2 == 0:
                            P.op("dve", TS(dg[:, ch, j, :], identf[:], dwf[:, ch, j:j + 1], ALU.mult),
                                 reads=["identf", "dwf"], writes=["dgdve"])
                        else:
                            P.op("act", ACT(dg[:, ch, j, :], identf[:], AF.Copy, scale=dwf[:, ch, j:j + 1]),
                                 reads=["identf", "dwf"], writes=["dgact"])
                hTs = [SB(s3, f"hT{i}", [128, 8, 512], BF16) for i in range(2)]
                kws = [SB(s3, f"kw{i}", [128, 4, 1024], BF16) for i in range(2)]
                vws = [SB(s3, f"vw{i}", [128, 8, 520], BF16) for i in range(2)]
                hcw = [SB(s3, f"hcw{i}", [128, 2, 542], BF16) for i in range(2)]
                ybw = [SB(s3, f"ybw{i}", [128, 2, 512], BF16) for i in range(2)]
                ybp = [SB(s3, f"ybp{i}", [128, 2, 512], BF16) for i in range(2)]
                qTm = SB(s3, "qTm", [128, 4, 4, 2, 128], BF16)
                tht = SB(s3, "tht", [128, 512], F32)
                hg = SB(s3, "hg", [128, 512], F32)
                cvh = SB(s3, "cvh", [128, 3, 2], F32)
                P.op("dve", TS(cvh[:], cv[:], 0.5, ALU.mult), reads=["cv"], writes=["cvh"])
                g2 = SB(s3, "g2", [128, 8, 512], BF16)
                PT = SB(s3, "PT", [128, 6, 8, 128], BF16)
                gat = SB(s3, "gat", [128, 4, 512], BF16)
                ytok = [SB(s3, f"ytok{i}", [128, 512], BF16) for i in range(2)]
                rinv = SB(s3, "rinv", [128, 8], F32)
                ycs = [SB(s3, f"ycs{i}", [128, 8, 512], BF16) for i in range(2)]
                vt = SB(s3, "vt", [128, 2, 512], F32)
                v2 = SB(s3, "v2", [128, 2, 512], F32)
                msb = SB(s3, "msb", [128, 512], F32)
                var = SB(s3, "var", [128, 512], F32)
                t1 = SB(s3, "t1", [128, 512], F32)
                lnb = SB(s3, "lnb", [128, 512], F32)
                acc = [PS(s3, f"pa{i}", [128, 512], F32) for i in range(2)]
                Sb = [PS(s3, f"Sb{i}", [128, 512], F32) for i in range(2)]
                Ot = [PS(s3, f"Ot{i}", [128, 512], F32) for i in range(2)]
                Tp = PS(s3, "Tp", [128, 1024], BF16)
                na = [0]
                nsb = [0]
                nyt = [0]
                pending = [[]]
                pend_tr = [None]
                P.op("pool", MS(qTm[:], 0.0), writes=["qTm"])

                def tile_geo(g):
                    if g < NTA:
                        return "A", g, RA, 0
                    gi = g - NTA
                    return "B", gi % TPS, RP, LA + (gi // TPS) * LP

                def load2(g):
                    b = g % 2
                    unit, i, R, ts0 = tile_geo(g)
                    L = R * 64
                    t0 = g * 512
                    P.dma("sp", DMA(hTs[b][:], hTd[:, :, t0:t0 + 512].rearrange("j p n -> p j n")), writes=[f"hT{b}"])
                    lo_w = max(0, 8 * i - 4)
                    hi_w = min(R, 8 * i + 12)
                    nk = (hi_w - lo_w) * 64
                    P.dma("sp", DMA(kws[b][:, :, 0:nk], kTd[:, :, ts0 + lo_w * 64:ts0 + hi_w * 64].rearrange("j p n -> p j n")),
                          writes=[f"kw{b}"])
                    P.dma("sp", DMA(vws[b][:, 0:nk // 128, :],
                                    vd[ts0 + lo_w * 64:ts0 + hi_w * 64, :].rearrange("(c p) d -> p c d", p=128)),
                          writes=[f"vw{b}"])
                    a0 = max(ts0, t0 - 15)
                    a1 = min(ts0 + L, t0 + 512 + 15)
                    if a0 > t0 - 15:
                        P.op("pool", MS(hcw[b][:, :, 0:15], 0.0), writes=[f"hcw{b}"])
                    if a1 < t0 + 512 + 15:
                        P.op("pool", MS(hcw[b][:, :, 527:542], 0.0), writes=[f"hcw{b}"])
                    P.dma("sp", DMA(hcw[b][:, :, a0 - (t0 - 15):a1 - (t0 - 15)], hcd[:, :, a0:a1].rearrange("c p n -> p c n")),
                          writes=[f"hcw{b}"])
                    if unit == "A":
                        if i % TPS == 0 and i > 0:
                            P.op("pool", TS(hcw[b][:, :, 0:15], hcw[b][:, :, 0:15], flg[:, 0:1], ALU.mult),
                                 reads=["flg", f"hcw{b}"], writes=[f"hcw{b}"])
                        if (i + 1) % TPS == 0 and i + 1 < NTA:
                            P.op("pool", TS(hcw[b][:, :, 527:542], hcw[b][:, :, 527:542], flg[:, 0:1], ALU.mult),
                                 reads=["flg", f"hcw{b}"], writes=[f"hcw{b}"])
                    P.dma("sp", DMA(ybw[b][:], Ybd2[0][:, :, t0:t0 + 512].rearrange("c p n -> p c n")), writes=[f"ybw{b}"])
                    if unit == "A":
                        P.dma("sp", DMA(ybp[b][:], Ybd2[1][:, :, t0:t0 + 512].rearrange("c p n -> p c n")), writes=[f"ybp{b}"])

                load2(0)
                for g in range(NT):
                    b = g % 2
                    unit, i, R, ts0 = tile_geo(g)
                    t0 = g * 512
                    if g + 1 < NT:
                        load2(g + 1)
                    hT = hTs[b]
                    kw = kws[b]
                    vw = vws[b]
                    yc = ycs[b]
                    ycn = f"ycs{b}"
                    lo_w = max(0, 8 * i - 4)
                    def nxt_acc():
                        na[0] += 1
                        return acc[na[0] % 2], f"pa{na[0] % 2}"

                    for m in range(4):
                        a, an = nxt_acc()
                        P.group("pe", [MM(a[:], w2[:, j, m * 128:(m + 1) * 128], hT[:, j, :], j == 0, j == 7) for j in range(8)],
                                reads=W2K + [f"hT{b}"], writes=[an])
                        av = a[:].rearrange("p (b q) -> p b q", b=4)
                        if m % 2 == 0:
                            P.op("act", ACT(qTm[0:64, m, :, 0, :], av[0:64], AF.Copy, scale=0.125), reads=[an], writes=["qTm"])
                            P.op("act", ACT(qTm[64:128, m, :, 1, :], av[64:128], AF.Copy, scale=0.125), reads=[an], writes=["qTm"])
                        else:
                            P.op("dve", TS(qTm[0:64, m, :, 0, :], av[0:64], 0.125, ALU.mult), reads=[an], writes=["qTm"])
                            P.op("dve", TS(qTm[64:128, m, :, 1, :], av[64:128], 0.125, ALU.mult), reads=[an], writes=["qTm"])

                    def f_gate(m):
                        a, an = nxt_acc()
                        P.group("pe", [MM(a[:], w2[:, j, 512 + m * 128:512 + (m + 1) * 128], hT[:, j, :], j == 0, j == 7)
                                       for j in range(8)], reads=W2K + [f"hT{b}"], writes=[an])
                        P.op("act", ACT(tht[:], a[:], AF.Tanh, scale=0.5), reads=[an], writes=["tht"])
                        P.op("act", ACT(hg[:], a[:], AF.Copy, scale=0.5), reads=[an], writes=["hg"])
                        P.op("dve", STT(g2[:, m, :], tht[:], 1.0, hg[:], ALU.add, ALU.mult), reads=["hg", "tht"], writes=[f"g2{m}"])

                    def f_yb():
                        for ch in range(2):
                            if unit == "A":
                                P.op("pool", TT(ybw[b][:, ch, :], ybw[b][:, ch, :], ybp[b][:, ch, :], ALU.add),
                                     reads=[f"ybw{b}", f"ybp{b}"], writes=[f"ybw{b}"])
                            P.op("pool", TT(yc[:, 4 + ch, :], ybw[b][:, ch, :], g2[:, 4 + ch, :], ALU.mult),
                                 reads=[f"ybw{b}", f"g2{4 + ch}"], writes=[ycn])

                    def f_conv(ch):
                        a, an = nxt_acc()
                        P.group("pe", [MM(a[:], dg[:, ch, j, :], hcw[b][:, ch, j:j + 512], j == 0, j == 30) for j in range(31)],
                                reads=["dgdve", "dgact", f"hcw{b}"], writes=[an])
                        P.op("act", ACT(vt[:, ch, :], a[:], AF.Identity, scale=0.5, bias=cv[:, 0, ch:ch + 1]),
                             reads=[an, "cv"], writes=[f"vt{ch}"])
                        P.op("act", ACT(v2[:, ch, :], a[:], AF.Square, scale=0.5, bias=cv[:, 0, ch:ch + 1]),
                             reads=[an, "cv"], writes=[f"v2{ch}"])

                    def f_ln_stats():
                        pm, pmn = nxt_acc()
                        P.group("pe", [MM(pm[:], onesf[:], vt[:, ch, :], ch == 0, ch == 1) for ch in range(2)],
                                reads=["onesf", "vt0", "vt1"], writes=[pmn])
                        pe2, pe2n = nxt_acc()
                        P.group("pe", [MM(pe2[:], onesf[:], v2[:, ch, :], ch == 0, ch == 1) for ch in range(2)],
                                reads=["onesf", "v20", "v21"], writes=[pe2n])
                        P.op("act", ACT(msb[:], pm[:], AF.Copy), reads=[pmn], writes=["msb"])
                        P.op("dve", TT(var[:], msb[:], msb[:], ALU.mult), reads=["msb"], writes=["var"])
                        P.op("dve", TT(var[:], pe2[:], var[:], ALU.subtract), reads=[pe2n, "var"], writes=["var"])

                    def ln_pieces(yc_, ycn_, t0_):
                        def p2():
                            P.op("act", ACT(var[:], var[:], AF.Sqrt, bias=EPS, scale=1.0), reads=["var"], writes=["var"])

                        def p3():
                            P.op("dve", RCP(var[:], var[:]), reads=["var"], writes=["var"])

                        def p4(ch):
                            def f():
                                P.op("dve", TT(t1[:], vt[:, ch, :], msb[:], ALU.subtract), reads=[f"vt{ch}", "msb"], writes=["t1"])
                                P.op("dve", TT(t1[:], t1[:], var[:], ALU.mult), reads=["t1", "var"], writes=["t1"])
                                P.op("act", ACT(lnb[:], t1[:], AF.Identity, scale=cvh[:, 1, ch:ch + 1], bias=cvh[:, 2, ch:ch + 1]),
                                     reads=["t1", "cvh"], writes=["lnb"])
                            return f

                        def p5(ch):
                            def f():
                                P.op("act", ACT(tht[:], lnb[:], AF.Tanh), reads=["lnb"], writes=["tht"])
                                P.op("dve", STT(t1[:], tht[:], 1.0, lnb[:], ALU.add, ALU.mult), reads=["tht", "lnb"], writes=["t1"])
                                P.op("pool", TT(yc_[:, 6 + ch, :], t1[:], g2[:, 6 + ch, :], ALU.mult),
                                     reads=["t1", f"g2{6 + ch}"], writes=[ycn_])
                            return f

                        def p6():
                            P.dma("sp", DMA(ycd[:, :, t0_:t0_ + 512].rearrange("j p n -> p j n"), yc_[:]), reads=[ycn_])
                        return [p2, p3, p4(0), p5(0), p4(1), p5(1), p6]

                    def f_gate_tok(sub):
                        a, an = nxt_acc()
                        P.group("pe", [MM(a[:], hT[:, j, sub * 128:(sub + 1) * 128], w2[:, j, 512:1024], j == 0, j == 7)
                                       for j in range(8)], reads=W2K + [f"hT{b}"], writes=[an])
                        P.op("act", ACT(tht[:], a[:], AF.Tanh, scale=0.5), reads=[an], writes=["tht"])
                        P.op("act", ACT(hg[:], a[:], AF.Copy, scale=0.5), reads=[an], writes=["hg"])
                        P.op("dve", STT(gat[:, sub, :], tht[:], 1.0, hg[:], ALU.add, ALU.mult), reads=["hg", "tht"], writes=[f"gat{sub}"])

                    for m in range(4):
                        f_gate_tok(m)
                    fillers = pending[0] + [lambda: f_gate(4), lambda: (f_gate(5), f_yb()), lambda: f_conv(0), lambda: f_conv(1),
                                            lambda: f_gate(6), lambda: f_gate(7), f_ln_stats]
                    pending[0] = []

                    for mq in range(4):
                        r0 = 8 * i + 2 * mq
                        q0 = mq * 128
                        key, chunks, interior = qb_desc(unit, r0, RA, RP)
                        nch = len(chunks)
                        for ci, (c, Dv) in enumerate(chunks):
                            koff = (2 * c - lo_w) * 64
                            dD = (Dv + 6) // 2
                            for hh in range(2):
                                nsb[0] += 1
                                S = Sb[nsb[0] % 2]
                                Sn = f"Sb{nsb[0] % 2}"
                                fns = []
                                if interior:
                                    fns.append(MM(S[:], ident[:], bt[:, INT_D.index(Dv), 4 * hh:4 * hh + 4, :].rearrange("p h q -> p (h q)"), True, False))
                                else:
                                    fns.append(MM(S[:], ident[:], bD[:, dD, 4 * hh:4 * hh + 4, :].rearrange("p h q -> p (h q)"), True, False))
                                    for h4 in range(4):
                                        fns.append(MM(S[:, h4 * 128:(h4 + 1) * 128], ident[:],
                                                      rm[:, bidx[(key[0], key[1], Dv)], :], False, False))
                                for pl in range(2):
                                    j = 2 * hh + pl
                                    fns.append(MM(S[:, pl * 256:(pl + 1) * 256], kw[:, j, koff:koff + 128],
                                                  qTm[:, j, mq, :, :].rearrange("p a q -> p (a q)"), False, pl == 1))
                                P.group("pe", fns, reads=[f"kw{b}", "qTm", "bt", "bD", "rm", "ident"], writes=[Sn])
                                P.op("act", ACT(PT[:, ci, 4 * hh:4 * hh + 4, :], S[:].rearrange("p (h q) -> p h q", h=4), AF.Exp),
                                     reads=[Sn], writes=[f"PT{ci}"])
                            if ci == 2 and pend_tr[0] is not None:
                                pend_tr[0]()
                                pend_tr[0] = None
                            if fillers and ci % 2 == 1:
                                fillers.pop(0)()
                        if fillers:
                            fillers.pop(0)()
                        nyt[0] += 1
                        yt = ytok[nyt[0] % 2]
                        ytn = f"ytok{nyt[0] % 2}"
                        for hh in range(2):
                            fns = []
                            for h4 in range(4):
                                h = 4 * hh + h4
                                for ci, (c, Dv) in enumerate(chunks):
                                    fns.append(MM(Ot[hh][:, h4 * 65:(h4 + 1) * 65], PT[:, ci, h, :],
                                                  vw[:, (2 * c - lo_w) // 2, h * 65:(h + 1) * 65], ci == 0, ci == nch - 1))
                            P.group("pe", fns, reads=[f"PT{ci}" for ci in range(nch)] + [f"vw{b}"], writes=[f"Ot{hh}"])
                            otv = Ot[hh][:, 0:260].rearrange("p (h e) -> p h e", e=65)
                            P.op("dve", RCP(rinv[:, 4 * hh:4 * hh + 4], otv[:, :, 64]), reads=[f"Ot{hh}"], writes=[f"rinv{hh}"])
                            for h4 in range(4):
                                h = 4 * hh + h4
                                P.op("dve", STT(yt[:, h * 64:(h + 1) * 64], otv[:, h4, 0:64], rinv[:, h:h + 1],
                                                gat[:, mq, h * 64:(h + 1) * 64], ALU.mult, ALU.mult),
                                     reads=[f"Ot{hh}", f"rinv{hh}", f"gat{mq}"], writes=[ytn])
                            if hh == 0 and fillers:
                                fillers.pop(0)()
                        def mk_tr(yt_, ytn_, yc_, ycn_, q0_):
                            def f():
                                P.group("pe", [TR(Tp[:, j * 128:(j + 1) * 128], yt_[:, j * 128:(j + 1) * 128], ident[:]) for j in range(4)],
                                        reads=[ytn_, "ident"], writes=["Tp"])
                                P.op("act", ACT(yc_[:, 0:4, q0_:q0_ + 128], Tp[:, 0:512].rearrange("p (j q) -> p j q", j=4), AF.Copy),
                                     reads=["Tp"], writes=[ycn_])
                            return f
                        pend_tr[0] = mk_tr(yt, ytn, yc, ycn, q0)
                    while fillers:
                        fillers.pop(0)()
                    pending[0] = ln_pieces(yc, ycn, t0)
                    if g == NT - 1:
                        pend_tr[0]()
                        pend_tr[0] = None
                        for f_ in pending[0]:
                            f_()
                        pending[0] = []
            P.barrier()
            if stop == "p2":
                break
            with ExitStack() as s4:
                w3 = SB(s4, "w3", [128, 8, 3072], BF16)
                wb = SB(s4, "wb", [128, 8, D], BF16)
                wo = SB(s4, "wo", [128, 8, D], BF16)

                def ld_w3(n):
                    P.dma("pool", DMA(w3[:, :, n * 512:(n + 1) * 512], wv[:, :, 3328 + n * 512:3328 + (n + 1) * 512]), writes=[f"w3_{n}"])

                def ld_wb(n):
                    P.dma("pool", DMA(wb[:, :, n * 512:(n + 1) * 512],
                                      w_abc[l].rearrange("(j p) c -> p j c", p=128)[:, :, n * 512:(n + 1) * 512]), writes=[f"wb{n}"])

                def ld_wo(n):
                    P.dma("pool", DMA(wo[:, :, n * 512:(n + 1) * 512],
                                      w_o[l].rearrange("(j p) c -> p j c", p=128)[:, :, n * 512:(n + 1) * 512]), writes=[f"wo{n}"])
                ld_w3(0); ld_wb(0); ld_w3(2); ld_w3(4); ld_w3(1); ld_wb(1); ld_w3(3); ld_w3(5); ld_wo(0); ld_wo(1)
                pg = SB(s4, "pg", [128, D], F32)
                P.dma("sp", DMA(pg[:], post_g[l]), writes=["pg"])
                hTs = [SB(s4, f"hT{i}", [128, 8, 512], BF16) for i in range(2)]
                ycs = [SB(s4, f"ycs{i}", [128, 8, 512], BF16) for i in range(2)]
                xts = [SB(s4, f"xt{i}", [128, 4, D], F32) for i in range(2)]
                ths = [SB(s4, f"ths{i}", [128, 512], F32) for i in range(2)]
                tmp = [SB(s4, f"tmp{i}", [128, 512], F32) for i in range(2)]
                macc = SB(s4, "macc", [128, 512], F32)
                mg = SB(s4, "mg", [128, 8, 512], BF16)
                junk = SB(s4, "junk", [128, 512], F32)
                ss = SB(s4, "ss", [128, 2], F32)
                rs = SB(s4, "rs", [128, 1], F32)
                ot = [SB(s4, f"ot{i}", [128, D], F32) for i in range(2)]
                pgt = [PS(s4, f"pg{i}", [128, 512], F32) for i in range(2)]
                ppt = [PS(s4, f"pp{i}", [128, 512], F32) for i in range(2)]
                pot = [PS(s4, f"po{i}", [128, 512], F32) for i in range(4)]
                ng = [0]
                branches = ((0, (0, 1, 2, 3)), (1, (4, 5)), (2, (6, 7)))

                def load3(g):
                    b = g % 2
                    t0 = g * 512
                    P.dma("sp", DMA(hTs[b][:], hTd[:, :, t0:t0 + 512].rearrange("j p n -> p j n")), writes=[f"hT{b}"])
                    P.dma("sp", DMA(ycs[b][:], ycd[:, :, t0:t0 + 512].rearrange("j p n -> p j n")), writes=[f"ycs{b}"])
                    P.dma("sp", DMA(xts[b][:], xin[t0:t0 + 512, :].rearrange("(s p) d -> p s d", p=128)), writes=[f"xt{b}"])

                load3(0)
                nst = [0]
                for g in range(NT):
                    b = g % 2
                    t0 = g * 512
                    if g + 1 < NT:
                        load3(g + 1)
                    hT = hTs[b]
                    yc = ycs[b]
                    xt = xts[b]
                    for m in range(8):
                        for bi_, (br, kcs) in enumerate(branches):
                            ng[0] += 1
                            gi = ng[0] % 2
                            pgm = pgt[gi]
                            ppm = ppt[gi]
                            col = br * D + m * 128
                            P.group("pe", [MM(pgm[:], w3[:, j, col:col + 128], hT[:, j, :], j == 0, j == 7) for j in range(8)],
                                    reads=[f"w3_{col // 512}", f"hT{b}"], writes=[f"pg{gi}"])
                            P.op("act", ACT(ths[gi][:], pgm[:], AF.Tanh, scale=0.5), reads=[f"pg{gi}"], writes=[f"ths{gi}"])
                            P.group("pe", [MM(ppm[:], wb[:, kc, m * 128:(m + 1) * 128], yc[:, kc, :], kc == kcs[0], kc == kcs[-1])
                                           for kc in kcs], reads=[f"wb{m // 4}", f"ycs{b}"], writes=[f"pp{gi}"])
                            if bi_ == 0:
                                P.op("dve", STT(macc[:], ths[gi][:], 1.0, ppm[:], ALU.add, ALU.mult),
                                     reads=[f"ths{gi}", f"pp{gi}"], writes=["macc"])
                            else:
                                P.op("dve", STT(tmp[gi][:], ths[gi][:], 1.0, ppm[:], ALU.add, ALU.mult),
                                     reads=[f"ths{gi}", f"pp{gi}"], writes=[f"tmp{gi}"])
                                if bi_ == 1:
                                    P.op("pool", TT(macc[:], macc[:], tmp[gi][:], ALU.add), reads=[f"tmp{gi}", "macc"], writes=["macc"])
                                else:
                                    P.op("pool", TT(mg[:, m, :], macc[:], tmp[gi][:], ALU.add), reads=[f"tmp{gi}", "macc"],
                                         writes=[f"mg{m}"])
                    for s in range(4):
                        o = ot[s % 2]
                        on = f"ot{s % 2}"
                        pos = []
                        for hf in range(2):
                            nst[0] += 1
                            pi = nst[0] % 4
                            po_ = pot[pi]
                            pos.append((po_, f"po{pi}"))
                            P.group("pe", [MM(po_[:], mg[:, m, s * 128:(s + 1) * 128], wo[:, m, hf * 512:(hf + 1) * 512], m == 0, m == 7)
                                           for m in range(8)], reads=[f"wo{hf}"] + [f"mg{m}" for m in range(8)], writes=[f"po{pi}"])
                        P.op("dve", MS(ss[:], 0.0), writes=["ss"])
                        for hf in range(2):
                            P.op("act", ACT(junk[:], pos[hf][0][:], AF.Square, accum_out=ss[:, hf:hf + 1]),
                                 reads=[pos[hf][1]], writes=["junk", "ss"])
                        P.op("dve", TT(rs[:], ss[:, 0:1], ss[:, 1:2], ALU.add), reads=["ss"], writes=["rs"])
                        P.op("act", ACT(rs[:], rs[:], AF.Sqrt, bias=EPS, scale=1.0 / (4.0 * D)), reads=["rs"], writes=["rs"])
                        P.op("dve", RCP(rs[:], rs[:]), reads=["rs"], writes=["rs"])
                        P.op("dve", TS(rs[:], rs[:], 0.5, ALU.mult), reads=["rs"], writes=["rs"])
                        for hf in range(2):
                            P.op("dve", STT(o[:, hf * 512:(hf + 1) * 512], pos[hf][0][:], rs[:, 0:1], pg[:, hf * 512:(hf + 1) * 512],
                                            ALU.mult, ALU.mult), reads=[pos[hf][1], "rs", "pg"], writes=[on])
                        P.op("pool", TT(o[:], o[:], xt[:, s, :], ALU.add), reads=[on, f"xt{b}"], writes=[on])
                        P.dma("sp", DMA(xout[t0 + s * 128:t0 + (s + 1) * 128, :], o[:]), reads=[on])
            P.barrier()
        P.finish()
    return nc, bidx


def host_consts(LA, NSUB, NB_, bidx, inputs, depth=DEPTH):
    f = np.float32
    LP = LA // NSUB
    d = {}
    d["w_in"] = np.ascontiguousarray(inputs["w_in"][:depth], dtype=f)
    d["w_abc"] = np.ascontiguousarray(np.concatenate(
        [inputs["w_a_out"][:depth], inputs["w_b_out"][:depth], inputs["w_c_out"][:depth]], axis=1), dtype=f)
    d["w_o"] = np.ascontiguousarray(inputs["w_o"][:depth], dtype=f)
    d["pre_g"] = np.ascontiguousarray(inputs["pre_norm_g"][:depth].reshape(depth, 8, 128).transpose(0, 2, 1), dtype=f)
    d["post_g"] = np.ascontiguousarray(np.broadcast_to(inputs["post_norm_g"][:depth][:, None, :], (depth, 128, D)), dtype=f)
    d["dww"] = np.ascontiguousarray(inputs["c_dw_w"][:depth].reshape(depth, 31, 2, 128).transpose(0, 3, 2, 1), dtype=f)
    cv = np.stack([inputs["c_dw_b"][:depth], inputs["c_norm_g"][:depth], inputs["c_norm_b"][:depth]], axis=1)
    d["cvec"] = np.ascontiguousarray(cv.reshape(depth, 3, 2, 128).transpose(0, 3, 1, 2), dtype=f)
    d["biasg"] = np.ascontiguousarray(
        np.stack([gather_bias(np.asarray(inputs["rel_pos_bias"][i], dtype=f)) for i in range(depth)]).reshape(depth, 128, -1))
    d["rmintd"] = np.ascontiguousarray(make_rowmask_int().reshape(128, -1))
    d["cmd"] = np.ascontiguousarray(make_colmask().reshape(128, -1))
    d["identd"] = np.eye(128, dtype=f)
    C, S = chan_consts()
    d["c64d"] = np.ascontiguousarray(np.stack([C, S], axis=1))
    cfgs = {(1, LA), (NSUB, LP)}
    if NB_:
        cfgs.add((NB_, LP))
    for (G, L) in cfgs:
        d[f"w1d{G}_{L}"] = w1_blockdiag(2 * G, L)
    for L in {LA, LP}:
        A = L // 128
        _, _, m2 = dft_consts(L)
        d[f"m2d{L}"] = np.ascontiguousarray(m2.reshape(128, A * 2, 256))
    return d


def mode_consts(bidx, mode):
    m = {}
    m["maskd"] = np.ascontiguousarray(make_rowmask(bidx, mode).reshape(128, -1))
    fl = np.zeros((128, 2), np.float32)
    fl[:, 0 if mode == "s" else 1] = 1.0
    m["flagd"] = fl
    return m


_CACHE = {}


def kernel(**inputs):
    inputs = {k: np.asarray(v) for k, v in inputs.items()}
    xp = inputs["x_prompt"]
    xs = inputs["x_sample"]
    B, S, _ = xp.shape
    DB, DS, _ = xs.shape
    n = 8
    NSUB = DS // S
    NB_ = 2
    assert DB * NB_ + (n - DB) * (NSUB + NB_) == B
    key = (DS, NSUB, NB_)
    if key not in _CACHE:
        _CACHE[key] = build(DS, NSUB, NB_)
    nc, bidx = _CACHE[key]
    consts = host_consts(DS, NSUB, NB_, bidx, inputs)
    ms = mode_consts(bidx, "s")
    mp = mode_consts(bidx, "p")
    in_maps = []
    plan = []
    nxt = 0
    for c in range(n):
        if c < DB:
            pidx = list(range(nxt, nxt + NB_))
            nxt += NB_
            xc = np.concatenate([xs[c]] + [xp[i] for i in pidx], axis=0)
            plan.append(("s", c, pidx))
            m = dict(consts, **ms)
        else:
            pidx = list(range(nxt, nxt + NSUB + NB_))
            nxt += NSUB + NB_
            xc = np.concatenate([xp[i] for i in pidx], axis=0)
            plan.append(("p", None, pidx))
            m = dict(consts, **mp)
        m["x"] = np.ascontiguousarray(xc, dtype=np.float32)
        in_maps.append(m)
    res = run_bass_kernel_spmd(nc, in_maps, core_ids=list(range(n)))
    yp = np.empty_like(xp, dtype=np.float32)
    ys = np.empty_like(xs, dtype=np.float32)
    for c in range(n):
        y = res.results[c]["y"]
        mode, sidx, pidx = plan[c]
        off = 0
        if mode == "s":
            ys[sidx] = y[0:DS]
            off = DS
        for i in pidx:
            yp[i] = y[off:off + S]
            off += S
    return (yp, ys)
```

```python
import os
import numpy as np
from contextlib import ExitStack
import concourse.bass as bass
import concourse.mybir as mybir
from concourse.bass_utils import run_bass_kernel_spmd

F32 = mybir.dt.float32
BF16 = mybir.dt.bfloat16
ALU = mybir.AluOpType
AF = mybir.ActivationFunctionType

EPOCH = 30000
NDMA_SEM = 20
D = 1024
DIN = 6400
EPS = 1e-6
NEG = -30000.0
DEPTH = 2


class Prog:
    ENGS = ("pe", "act", "dve", "pool", "sp")

    def __init__(self, nc, stack):
        self.nc = nc
        self.stack = stack
        self.ops = {e: [] for e in self.ENGS}
        self.cnt = {e: 0 for e in self.ENGS}
        self.sems = {e: [] for e in self.ENGS}
        self.seen = {e: {} for e in self.ENGS}
        self.regions = {}
        self.dma_sems = {}
        self.dma_val = {}
        self.dma_rr = {"sp": 0, "pool": 0, "act": 0}
        self.nsem = {"sp": NDMA_SEM, "pool": 6}
        for q in ("sp", "pool"):
            self.dma_sems[q] = [self._mksem(f"d{q}{i}") for i in range(self.nsem[q])]

    def _mksem(self, name):
        s = self.stack.enter_context(self.nc.semaphore(name))
        self.dma_val[name] = 0
        return (name, s)

    def _eng_sem(self, e, n):
        idx = (n - 1) // EPOCH
        while len(self.sems[e]) <= idx:
            nm = f"s{e}{len(self.sems[e])}"
            self.sems[e].append((nm, self.stack.enter_context(self.nc.semaphore(nm))))
        return self.sems[e][idx], n - idx * EPOCH

    def _wait(self, eng, ticket):
        if ticket is None:
            return
        if ticket[0] == "eng":
            _, f, n = ticket
            (nm, s), v = self._eng_sem(f, n)
        else:
            _, (nm, s), v = ticket
        if self.seen[eng].get(nm, 0) >= v:
            return
        self.seen[eng][nm] = v
        self.ops[eng].append(lambda e, s=s, v=v: e.wait_ge(s, v))

    def _deps(self, eng, reads, writes):
        for k in reads:
            r = self.regions.get(k)
            if r is not None:
                for t in r["w"].values():
                    self._wait(eng, t)
        for k in writes:
            r = self.regions.get(k)
            if r is not None:
                for t in r["w"].values():
                    self._wait(eng, t)
                for t in r["r"].values():
                    self._wait(eng, t)

    def _record(self, ticket, reads, writes):
        rk = ticket[1] if ticket[0] == "eng" else ticket[1][0]
        for k in writes:
            r = self.regions.setdefault(k, {"w": {}, "r": {}})
            r["w"][rk] = ticket
            r["r"] = {}
        for k in reads:
            if k in writes:
                continue
            r = self.regions.setdefault(k, {"w": {}, "r": {}})
            r["r"][rk] = ticket

    def op(self, eng, fn, reads=(), writes=()):
        return self.group(eng, [fn], reads, writes)

    def group(self, eng, fns, reads=(), writes=()):
        self._deps(eng, reads, writes)
        for fn in fns[:-1]:
            self.ops[eng].append(lambda e, fn=fn: fn(e))
        self.cnt[eng] += 1
        n = self.cnt[eng]
        (nm, s), v = self._eng_sem(eng, n)
        fn = fns[-1]
        self.ops[eng].append(lambda e, fn=fn, s=s: fn(e).then_inc(s, 1))
        t = ("eng", eng, n)
        self._record(t, reads, writes)
        return t

    def dma(self, q, fn, reads=(), writes=()):
        self._deps(q, reads, writes)
        i = self.dma_rr[q]
        self.dma_rr[q] = (i + 1) % self.nsem[q]
        nm, s = self.dma_sems[q][i]
        prev = self.dma_val[nm]
        if prev > 0:
            self._wait(q, ("dma", (nm, s), prev))
        v = prev + 16
        self.dma_val[nm] = v
        self.ops[q].append(lambda e, fn=fn, s=s: fn(e).then_inc(s, 16))
        t = ("dma", (nm, s), v)
        self._record(t, reads, writes)
        return t

    def barrier(self):
        for e in self.ENGS:
            for f in ("pe", "act", "dve", "pool"):
                if self.cnt[f] > 0:
                    self._wait(e, ("eng", f, self.cnt[f]))
            for q in ("sp", "pool"):
                for nm, s in self.dma_sems[q]:
                    v = self.dma_val[nm]
                    if v > 0:
                        self._wait(e, ("dma", (nm, s), v))
        self.regions = {}

    def finish(self):
        self.barrier()
        nc = self.nc
        with nc.Block() as blk:
            @blk.tensor
            def _(e):
                for f in self.ops["pe"]:
                    f(e)

            @blk.scalar
            def _(e):
                for f in self.ops["act"]:
                    f(e)

            @blk.vector
            def _(e):
                for f in self.ops["dve"]:
                    f(e)

            @blk.gpsimd
            def _(e):
                for f in self.ops["pool"]:
                    f(e)

            @blk.sync
            def _(e):
                for f in self.ops["sp"]:
                    f(e)


def MM(out, lhsT, rhs, start, stop):
    return lambda e: e.matmul(out, lhsT=lhsT, rhs=rhs, start=start, stop=stop)


def TR(out, in_, ident):
    return lambda e: e.transpose(out, in_, ident)


def ACT(out, in_, func, **kw):
    return lambda e: e.activation(out=out, in_=in_, func=func, **kw)


def TT(out, in0, in1, op):
    return lambda e: e.tensor_tensor(out=out, in0=in0, in1=in1, op=op)


def TS(out, in0, s1, op0):
    return lambda e: e.tensor_scalar(out=out, in0=in0, scalar1=s1, scalar2=None, op0=op0)


def STT(out, in0, scalar, in1, op0, op1):
    return lambda e: e.scalar_tensor_tensor(out=out, in0=in0, scalar=scalar, in1=in1, op0=op0, op1=op1)


def CP(out, in_):
    return lambda e: e.tensor_copy(out=out, in_=in_)


def RCP(out, in_):
    return lambda e: e.reciprocal(out=out, in_=in_)


def MS(ap, v):
    return lambda e: e.memset(ap, v)


def DMA(out, in_):
    return lambda e: e.dma_start(out=out, in_=in_)


def r_start(r, R):
    return min(max(r - 4, 0), R - 8)


def qblock_info(r0, R):
    s0 = r_start(r0, R) - r0
    s1 = r_start(r0 + 1, R) - r0
    lo = r_start(r0, R)
    hi = r_start(r0 + 1, R) + 8
    chunks = list(range(lo // 2, (hi + 1) // 2))
    return (s0, s1), [(c, 2 * c - r0) for c in chunks]


def unitA_qblock(r0, RA, RP):
    ts_, chs = qblock_info(r0, RA)
    base = (r0 // RP) * RP
    tp_, chp = qblock_info(r0 - base, RP)
    ds = {Dv for _, Dv in chs} | {Dv for _, Dv in chp}
    return (ts_, tp_), [((r0 + Dv) // 2, Dv) for Dv in sorted(ds)]


INT_T = (-4, -3)
INT_D = (-4, -2, 0, 2, 4)


def qb_desc(unit, r0, RA, RP):
    if unit == "A":
        key, ch = unitA_qblock(r0, RA, RP)
    else:
        t, ch = qblock_info(r0, RP)
        key = (t, t)
    interior = key == (INT_T, INT_T) and tuple(d for _, d in ch) == INT_D
    return key, ch, interior


def enum_rowmask(LA, NSUB, NB):
    idx = {}
    RA = LA // 64
    RP = RA // NSUB
    for r0 in range(0, RA, 2):
        key, ch, interior = qb_desc("A", r0, RA, RP)
        if not interior:
            for c, Dv in ch:
                idx.setdefault((key[0], key[1], Dv), len(idx))
    if NB:
        for r0 in range(0, RP, 2):
            key, ch, interior = qb_desc("B", r0, RA, RP)
            if not interior:
                for c, Dv in ch:
                    idx.setdefault((key[0], key[1], Dv), len(idx))
    return idx


def rowmask_block(typ, Dv):
    m = np.full((128, 128), NEG, np.float32)
    for krp in range(2):
        for qrp in range(2):
            s = typ[qrp]
            if s <= Dv + krp < s + 8:
                m[krp * 64:(krp + 1) * 64, qrp * 64:(qrp + 1) * 64] = 0.0
    return m


def make_rowmask(idx, mode):
    m = np.zeros((128, max(len(idx), 1), 128), np.float32)
    for (ts_, tp_, Dv), i in idx.items():
        m[:, i, :] = rowmask_block(ts_ if mode == "s" else tp_, Dv)
    return m


def make_rowmask_int():
    return np.stack([rowmask_block(INT_T, Dv) for Dv in INT_D], axis=1)


def make_colmask():
    col = np.arange(64)
    cs = np.clip(col - 8, 0, 48)
    colv = (col[:, None] >= cs[None, :]) & (col[:, None] < cs[None, :] + 16)
    m = np.full((128, 7, 128), NEG, np.float32)
    for d in range(7):
        Dv = 2 * d - 6
        for krp in range(2):
            for qrp in range(2):
                if abs(Dv + krp - qrp) <= 7:
                    m[krp * 64:(krp + 1) * 64, d, qrp * 64:(qrp + 1) * 64] = np.where(colv, 0.0, NEG)
    return m


def gather_bias(rpb):
    col = np.arange(64)
    dc = np.clip(col[:, None] - col[None, :] + 15, 0, 30)
    out = np.zeros((128, 7, 8, 128), np.float32)
    for d in range(7):
        Dv = 2 * d - 6
        for krp in range(2):
            for qrp in range(2):
                dr = int(np.clip(Dv + krp - qrp + 7, 0, 14))
                blk = rpb[:, dr][:, dc]
                out[krp * 64:(krp + 1) * 64, d, :, qrp * 64:(qrp + 1) * 64] = blk.transpose(1, 0, 2)
    return out


def dft_consts(L):
    A = L // 128
    s = 1.0 / np.sqrt(64.0 * L)
    a = np.arange(A)
    ang1 = 2 * np.pi * np.outer(a, a) / A
    w1c = (np.cos(ang1) * s).astype(np.float32)
    w1s = (-np.sin(ang1) * s).astype(np.float32)
    b = np.arange(128, dtype=np.float64)
    ka = np.arange(A, dtype=np.float64)
    kb = np.arange(128, dtype=np.float64)
    k = ka[:, None] + A * kb[None, :]
    ang2 = 2 * np.pi * b[:, None, None] * k[None] / L
    mr = np.cos(ang2)
    mi = -np.sin(ang2)
    m2 = np.zeros((128, A, 2, 256), np.float32)
    m2[:, :, 0, 0:128] = mr
    m2[:, :, 0, 128:256] = mi
    m2[:, :, 1, 0:128] = -mi
    m2[:, :, 1, 128:256] = mr
    return w1c, w1s, m2


def w1_blockdiag(G, L):
    A = L // 128
    w1c, w1s, _ = dft_consts(L)
    K = G * A
    out = np.zeros((K, 2, K), np.float32)
    for g in range(G):
        out[g * A:(g + 1) * A, 0, g * A:(g + 1) * A] = w1c
        out[g * A:(g + 1) * A, 1, g * A:(g + 1) * A] = w1s
    return out


def chan_consts():
    c = np.arange(64)
    ang = 2 * np.pi * np.outer(c, c) / 64
    C = np.zeros((128, 128), np.float32)
    S = np.zeros((128, 128), np.float32)
    for g in range(2):
        C[g * 64:(g + 1) * 64, g * 64:(g + 1) * 64] = np.cos(ang)
        S[g * 64:(g + 1) * 64, g * 64:(g + 1) * 64] = np.sin(ang)
    return C, S


def build(LA, NSUB, NB_, depth=DEPTH, stop=None):
    LP = LA // NSUB
    T = LA + NB_ * LP
    NT = T // 512
    RA = LA // 64
    RP = LP // 64
    NTA = LA // 512
    TPS = LP // 512
    bidx = enum_rowmask(LA, NSUB, NB_)
    NB = max(len(bidx), 1)
    Ls = sorted({LA, LP})
    groups = [(0, 1, LA, 0, 0), (0, NSUB, LP, 1, 1)]
    if NB_:
        groups.append((LA, NB_, LP, 0, None))
    gcfg = sorted({(G, L) for _, G, L, _, _ in groups})

    nc = bass.Bass("TRN2", target_bir_lowering=False)
    dram = lambda n, sh, dt, kind="Internal": nc.dram_tensor(n, sh, dt, kind=kind).ap()
    x_in = dram("x", [T, D], F32, "ExternalInput")
    y_out = dram("y", [T, D], F32, "ExternalOutput")
    w_in = dram("w_in", [depth, D, DIN], F32, "ExternalInput")
    w_abc = dram("w_abc", [depth, D, D], F32, "ExternalInput")
    w_o = dram("w_o", [depth, D, D], F32, "ExternalInput")
    pre_g = dram("pre_g", [depth, 128, 8], F32, "ExternalInput")
    post_g = dram("post_g", [depth, 128, D], F32, "ExternalInput")
    dww = dram("dww", [depth, 128, 2, 31], F32, "ExternalInput")
    cvec = dram("cvec", [depth, 128, 3, 2], F32, "ExternalInput")
    biasg = dram("biasg", [depth, 128, 7 * 8 * 128], F32, "ExternalInput")
    rmintd = dram("rmintd", [128, 5 * 128], F32, "ExternalInput")
    cmd = dram("cmd", [128, 7 * 128], F32, "ExternalInput")
    maskd = dram("maskd", [128, NB * 128], F32, "ExternalInput")
    identd = dram("identd", [128, 128], F32, "ExternalInput")
    c64d = dram("c64d", [128, 2, 128], F32, "ExternalInput")
    w1d = {(G, L): dram(f"w1d{G}_{L}", [2 * G * L // 128, 2, 2 * G * L // 128], F32, "ExternalInput") for (G, L) in gcfg}
    flagd = dram("flagd", [128, 2], F32, "ExternalInput")
    m2d = {L: dram(f"m2d{L}", [128, (L // 128) * 2, 256], F32, "ExternalInput") for L in Ls}
    xmid = dram("xmid", [T, D], F32)
    m2b = {L: dram(f"m2b{L}", [128, (L // 128) * 2, 256], BF16) for L in Ls}
    hTd = dram("hTd", [8, 128, T], BF16)
    kTd = dram("kTd", [4, 128, T], BF16)
    vd = dram("vd", [T, 520], BF16)
    Ud = dram("Ud", [T, 256], BF16)
    Zdg = [dram(f"Zd{i}", [2, G_ * L_ * 256], BF16) for i, (_, G_, L_, _, _) in enumerate(groups)]
    Ybd2 = [dram(f"Ybd{i}", [2, 128, T], BF16) for i in range(2)]
    hcd = dram("hcd", [2, 128, T], BF16)
    ycd = dram("ycd", [8, 128, T], BF16)

    with ExitStack() as st:
        P = Prog(nc, st)

        uid = [0]

        def SB(stk, n, sh, dt):
            uid[0] += 1
            return stk.enter_context(nc.sbuf_tensor(f"{n}_{uid[0]}", sh, dt))

        def PS(stk, n, sh, dt):
            uid[0] += 1
            return stk.enter_context(nc.psum_tensor(f"{n}_{uid[0]}", sh, dt))

        ident = SB(st, "ident", [128, 128], BF16)
        onesb = SB(st, "onesb", [128, 128], BF16)
        onesf = SB(st, "onesf", [128, 128], F32)
        c64 = SB(st, "c64", [128, 2, 128], BF16)
        P.dma("pool", DMA(ident[:], identd), writes=["ident"])
        P.dma("pool", DMA(c64[:], c64d), writes=["c64"])
        flg = SB(st, "flg", [128, 2], F32)
        P.dma("sp", DMA(flg[:], flagd), writes=["flg"])
        P.op("dve", MS(onesb[:], 1.0), writes=["onesb"])
        P.op("dve", MS(onesf[:], 1.0 / 256.0), writes=["onesf"])

        for l in range(depth):
            xin = x_in if l == 0 else xmid
            xout = y_out if l == depth - 1 else xmid
            if depth == 1:
                xout = y_out
            wv = w_in[l].rearrange("(j p) c -> p j c", p=128)
            P.barrier()
            with ExitStack() as s1:
                w1 = SB(s1, "w1", [128, 8, 1792], BF16)
                for (dst, src, n) in ((0, 512, 512), (512, 1024, 512), (1024, 2048, 256),
                                      (1280, 2560, 256), (1536, 2816, 256)):
                    P.dma("pool", DMA(w1[:, :, dst:dst + n], wv[:, :, src:src + n]), writes=[f"w1_{dst}"])
                W1K = ["w1_0", "w1_512", "w1_1024", "w1_1280", "w1_1536"]
                if l == 0:
                    cst = [SB(s1, f"cst{i}", [128, 8, 256], BF16) for i in range(2)]
                    kk = 0
                    for L_ in Ls:
                        for h2 in range(0, 2 * (L_ // 128), 8):
                            P.dma("pool", DMA(cst[kk % 2][:], m2d[L_][:, h2:h2 + 8, :]), writes=[f"cst{kk % 2}"])
                            P.dma("pool", DMA(m2b[L_][:, h2:h2 + 8, :], cst[kk % 2][:]), reads=[f"cst{kk % 2}"])
                            kk += 1
                g1 = SB(s1, "g1", [128, 8], F32)
                P.dma("sp", DMA(g1[:], pre_g[l]), writes=["g1"])
                xts = [SB(s1, f"xt{i}", [128, 4, D], F32) for i in range(2)]
                hb = SB(s1, "hb", [128, 4, D], BF16)
                junk = SB(s1, "junk", [128, D], F32)
                ss = SB(s1, "ss", [128, 4], F32)
                rs = SB(s1, "rs", [128, 4], F32)
                hTs = [SB(s1, f"hT{i}", [128, 8, 512], BF16) for i in range(2)]
                kTs = SB(s1, "kTs", [128, 4, 512], BF16)
                vts = SB(s1, "vts", [128, 4, 520], BF16)
                uts = SB(s1, "uts", [128, 4, 256], BF16)
                th = SB(s1, "th", [128, 2, 512], F32)
                hcs = SB(s1, "hcs", [128, 2, 512], BF16)
                tps = [PS(s1, f"tp{i}", [128, 1024], BF16) for i in range(2)]
                acc = [PS(s1, f"acc{i}", [128, 512], F32) for i in range(6)]
                na = [0]

                def nacc():
                    na[0] += 1
                    i = na[0] % 6
                    return acc[i], f"acc{i}"

                def load_x(g):
                    b = g % 2
                    P.dma("sp", DMA(xts[b][:], xin[g * 512:(g + 1) * 512, :].rearrange("(s p) d -> p s d", p=128)),
                          writes=[f"xt{b}"])

                P.op("pool", MS(vts[:], 1.0), writes=["vts"])
                hbs = [hb, SB(s1, "hb2", [128, 4, D], BF16)]

                def norm(g):
                    b = g % 2
                    xt = xts[b]
                    hbb = hbs[b]
                    P.op("dve", MS(ss[:], 0.0), writes=["ss"])
                    for s in range(4):
                        P.op("act", ACT(junk[:], xt[:, s, :], AF.Square, accum_out=ss[:, s:s + 1]),
                             reads=[f"xt{b}"], writes=["junk", "ss"])
                    P.op("act", ACT(rs[:], ss[:], AF.Sqrt, bias=EPS, scale=1.0 / D), reads=["ss"], writes=["rs"])
                    P.op("dve", RCP(rs[:], rs[:]), reads=["rs"], writes=["rs"])
                    for s in range(4):
                        if s % 2 == 0:
                            P.op("act", ACT(hbb[:, s, :], xt[:, s, :], AF.Copy, scale=rs[:, s:s + 1]),
                                 reads=[f"xt{b}", "rs"], writes=[f"hb{b}_{s}"])
                        else:
                            P.op("dve", TS(hbb[:, s, :], xt[:, s, :], rs[:, s:s + 1], ALU.mult),
                                 reads=[f"xt{b}", "rs"], writes=[f"hb{b}_{s}"])

                load_x(0)
                if NT > 1:
                    load_x(1)
                norm(0)
                for g in range(NT):
                    b = g % 2
                    t0 = g * 512
                    hT = hTs[b]
                    hbb = hbs[b]
                    for j in range(8):
                        tp = tps[j % 2]
                        P.group("pe", [TR(tp[:, s * 128:(s + 1) * 128], hbb[:, s, j * 128:(j + 1) * 128], ident[:])
                                       for s in range(4)],
                                reads=[f"hb{b}_{s}" for s in range(4)] + ["ident"], writes=[f"tp{j % 2}"])
                        if j % 2 == 0:
                            P.op("dve", TS(hT[:, j, :], tp[:, 0:512], g1[:, j:j + 1], ALU.mult),
                                 reads=[f"tp{j % 2}", "g1"], writes=[f"hT{b}"])
                        else:
                            P.op("act", ACT(hT[:, j, :], tp[:, 0:512], AF.Copy, scale=g1[:, j:j + 1]),
                                 reads=[f"tp{j % 2}", "g1"], writes=[f"hT{b}"])
                    if g + 1 < NT:
                        norm(g + 1)
                    if g + 2 < NT:
                        load_x(g + 2)
                    P.dma("sp", DMA(hTd[:, :, t0:t0 + 512].rearrange("j p n -> p j n"), hT[:]), reads=[f"hT{b}"])
                    for m in range(4):
                        a, an = nacc()
                        P.group("pe", [MM(a[:], w1[:, j, m * 128:(m + 1) * 128], hT[:, j, :], j == 0, j == 7)
                                       for j in range(8)], reads=["w1_0", f"hT{b}"], writes=[an])
                        if m % 2 == 0:
                            P.op("act", ACT(kTs[:, m, :], a[:], AF.Copy), reads=[an], writes=["kTs"])
                        else:
                            P.op("dve", CP(kTs[:, m, :], a[:]), reads=[an], writes=["kTs"])
                    P.dma("sp", DMA(kTd[:, :, t0:t0 + 512].rearrange("j p n -> p j n"), kTs[:]), reads=["kTs"])
                    for s in range(4):
                        a, an = nacc()
                        P.group("pe", [MM(a[:], hT[:, j, s * 128:(s + 1) * 128], w1[:, j, 512:1024], j == 0, j == 7)
                                       for j in range(8)], reads=["w1_512", f"hT{b}"], writes=[an])
                        vdst = vts[:, s, :].rearrange("p (h e) -> p h e", e=65)[:, :, 0:64]
                        vsrc = a[:].rearrange("p (h e) -> p h e", e=64)
                        if s % 2 == 0:
                            P.op("dve", CP(vdst, vsrc), reads=[an], writes=["vts"])
                        else:
                            P.op("act", ACT(vdst, vsrc, AF.Copy), reads=[an], writes=["vts"])
                    P.dma("sp", DMA(vd[t0:t0 + 512, :].rearrange("(s p) d -> p s d", p=128), vts[:]), reads=["vts"])
                    for s in range(4):
                        a, an = nacc()
                        P.group("pe", [MM(a[:, 0:256], hT[:, j, s * 128:(s + 1) * 128], w1[:, j, 1024:1280], j == 0, j == 7)
                                       for j in range(8)], reads=["w1_1024", f"hT{b}"], writes=[an])
                        P.op("dve", CP(uts[:, s, :], a[:, 0:256]), reads=[an], writes=["uts"])
                    P.dma("sp", DMA(Ud[t0:t0 + 512, :].rearrange("(s p) d -> p s d", p=128), uts[:]), reads=["uts"])
                    for ch in range(2):
                        a, an = nacc()
                        P.group("pe", [MM(a[:], w1[:, j, 1536 + ch * 128:1536 + (ch + 1) * 128], hT[:, j, :], j == 0, j == 7)
                                       for j in range(8)], reads=["w1_1536", f"hT{b}"], writes=[an])
                        P.op("act", ACT(th[:, ch, :], a[:], AF.Tanh, scale=0.5), reads=[an], writes=[f"th{ch}"])
                        a2, an2 = nacc()
                        P.group("pe", [MM(a2[:], w1[:, j, 1280 + ch * 128:1280 + (ch + 1) * 128], hT[:, j, :], j == 0, j == 7)
                                       for j in range(8)], reads=["w1_1280", f"hT{b}"], writes=[an2])
                        P.op("dve", STT(hcs[:, ch, :], th[:, ch, :], 1.0, a2[:], ALU.add, ALU.mult),
                             reads=[an2, f"th{ch}"], writes=["hcs"])
                    P.dma("sp", DMA(hcd[:, :, t0:t0 + 512].rearrange("c p n -> p c n"), hcs[:]), reads=["hcs"])
            P.barrier()
            if stop == "p1":
                break
            KMAX = max(G * L // 128 for _, G, L, _, _ in groups)
            TGMAX = max(G * L for _, G, L, _, _ in groups)
            with ExitStack() as s2:
                usb = SB(s2, "usb", [2 * KMAX, 16384], BF16)
                zst = [SB(s2, f"zst{i}", [2 * KMAX, 2, 4096], BF16) for i in range(2)]
                zp = [PS(s2, f"zp{i}", [128, 512], F32) for i in range(4)]
                for gi_, (ts0, G, L, obuf, fcol) in enumerate(groups):
                    A = L // 128
                    K = G * A
                    K2 = 2 * K
                    TG = G * L
                    w1c = SB(s2, f"w1c{gi_}", [K2, 2, K2], BF16)
                    P.dma("pool", DMA(w1c[:], w1d[(G, L)]), writes=[f"w1c{gi_}"])
                    udv = Ud[ts0:ts0 + TG, :].rearrange("(a b) c -> a (b c)", b=128)
                    for gq in range(4):
                        for hb in range(2):
                            P.dma("sp", DMA(usb[hb * K:(hb + 1) * K, gq * 4096:(gq + 1) * 4096],
                                            udv[:, hb * 16384 + gq * 4096:hb * 16384 + (gq + 1) * 4096]), writes=[f"usb{gq}_{hb}"])
                    for gq in range(4):
                        zs = zst[gq % 2]
                        for n8 in range(8):
                            n = gq * 8 + n8
                            for r in range(2):
                                pi = (n8 % 2) * 2 + r
                                P.group("pe", [MM(zp[pi][0:K2, :], w1c[:, r, :], usb[0:K2, n * 512:(n + 1) * 512], True, True)],
                                        reads=[f"w1c{gi_}", f"usb{gq}_0", f"usb{gq}_1"], writes=[f"zp{pi}"])
                                if r == 0:
                                    P.op("act", ACT(zs[0:K2, r, n8 * 512:(n8 + 1) * 512], zp[pi][0:K2, :], AF.Copy),
                                         reads=[f"zp{pi}"], writes=[f"zst{gq % 2}"])
                                else:
                                    P.op("dve", CP(zs[0:K2, r, n8 * 512:(n8 + 1) * 512], zp[pi][0:K2, :]),
                                         reads=[f"zp{pi}"], writes=[f"zst{gq % 2}"])
                        for r in range(2):
                            zdv = Zdg[gi_][r, :].rearrange("(a n) -> a n", a=K)
                            for hb in range(2):
                                P.dma("sp", DMA(zdv[:, hb * 16384 + gq * 4096:hb * 16384 + (gq + 1) * 4096], zs[hb * K:(hb + 1) * K, r, :]),
                                      reads=[f"zst{gq % 2}"], writes=[f"Zd{gi_}_{r}_{hb}"])
            P.barrier()
            with ExitStack() as s2:
                AS = LA // 128
                AP_ = LP // 128
                m2s = SB(s2, "m2s", [128, AS * 2, 256], BF16)
                m2p = SB(s2, "m2p", [128, AP_ * 2, 256], BF16)
                zb = [SB(s2, f"zb{r}", [128, KMAX, 256], BF16) for r in range(2)]
                xr = SB(s2, "xr", [128, TGMAX], BF16)
                xi = SB(s2, "xi", [128, TGMAX], BF16)
                yst = SB(s2, "yst", [128, TGMAX], BF16)
                xp = [PS(s2, f"xp{i}", [128, 512], F32) for i in range(3)]
                yp = [PS(s2, f"yp{i}", [128, 512], F32) for i in range(2)]
                for (mt, Lm, nm) in ((m2s, LA, "m2s"), (m2p, LP, "m2p")):
                    for h2 in range(0, 2 * (Lm // 128), 32):
                        h3 = min(2 * (Lm // 128), h2 + 32)
                        P.dma("sp", DMA(mt[:, h2:h3, :], m2b[Lm][:, h2:h3, :]), writes=[f"{nm}_{h2}"])
                for gi_, (ts0, G, L, obuf, fcol) in enumerate(groups):
                    A = L // 128
                    K = G * A
                    TG = G * L
                    m2, m2n = (m2s, "m2s") if (L == LA and gi_ == 0) else (m2p, "m2p")
                    for a0_ in range(0, K, 16):
                        a1_ = min(K, a0_ + 16)
                        for r in range(2):
                            zv = Zdg[gi_][r, :].rearrange("(a b c) -> b a c", a=K, b=128, c=256)
                            P.dma("sp", DMA(zb[r][:, a0_:a1_, :], zv[:, a0_:a1_, :]), writes=[f"zb{r}_{a0_}"])
                    for cc in range(2):
                        for idx_ in range(K):
                            ka = idx_ % A
                            xpp = xp[idx_ % 3]
                            P.group("pe", [MM(xpp[:, 0:256], zb[0][:, idx_, cc * 128:(cc + 1) * 128], m2[:, ka * 2 + 0, :], True, False),
                                           MM(xpp[:, 0:256], zb[1][:, idx_, cc * 128:(cc + 1) * 128], m2[:, ka * 2 + 1, :], False, True)],
                                    reads=[f"zb0_{(idx_ // 16) * 16}", f"zb1_{(idx_ // 16) * 16}", f"{m2n}_{((ka * 2) // 32) * 32}"],
                                    writes=[f"xp{idx_ % 3}"])
                            xrv = xr[:, idx_ * 128:(idx_ + 1) * 128]
                            xiv = xi[:, idx_ * 128:(idx_ + 1) * 128]
                            if idx_ % 2 == 0:
                                P.op("act", ACT(xrv, xpp[:, 0:128], AF.Copy), reads=[f"xp{idx_ % 3}"], writes=["xr"])
                                P.op("act", ACT(xiv, xpp[:, 128:256], AF.Copy), reads=[f"xp{idx_ % 3}"], writes=["xi"])
                            else:
                                P.op("dve", CP(xrv, xpp[:, 0:128]), reads=[f"xp{idx_ % 3}"], writes=["xr"])
                                P.op("dve", CP(xiv, xpp[:, 128:256]), reads=[f"xp{idx_ % 3}"], writes=["xi"])
                        nkb = 512 // A
                        cnt_ = 0
                        for g_ in range(G):
                            for n in range(L // 512):
                                ypp = yp[cnt_ % 2]
                                ypn = f"yp{cnt_ % 2}"
                                xrs = xr[:, g_ * L:(g_ + 1) * L].rearrange("p (a kb) -> p kb a", a=A)[:, n * nkb:(n + 1) * nkb, :]
                                xis = xi[:, g_ * L:(g_ + 1) * L].rearrange("p (a kb) -> p kb a", a=A)[:, n * nkb:(n + 1) * nkb, :]
                                P.group("pe", [MM(ypp[:].rearrange("p (kb a) -> p kb a", a=A), c64[:, 0, :], xrs, True, False),
                                               MM(ypp[:].rearrange("p (kb a) -> p kb a", a=A), c64[:, 1, :], xis, False, True)],
                                        reads=["c64", "xr", "xi"], writes=[ypn])
                                yo = yst[:, g_ * L + n * 512:g_ * L + (n + 1) * 512]
                                if fcol is None:
                                    if cnt_ % 2 == 0:
                                        P.op("act", ACT(yo, ypp[:], AF.Copy), reads=[ypn], writes=["yst"])
                                    else:
                                        P.op("dve", CP(yo, ypp[:]), reads=[ypn], writes=["yst"])
                                else:
                                    if cnt_ % 2 == 0:
                                        P.op("act", ACT(yo, ypp[:], AF.Copy, scale=flg[:, fcol:fcol + 1]), reads=[ypn, "flg"], writes=["yst"])
                                    else:
                                        P.op("dve", TS(yo, ypp[:], flg[:, fcol:fcol + 1], ALU.mult), reads=[ypn, "flg"], writes=["yst"])
                                cnt_ += 1
                        P.dma("sp", DMA(Ybd2[obuf][cc, :, ts0:ts0 + TG], yst[:, 0:TG]), reads=["yst"])
            P.barrier()
            if stop == "fnet":
                break
            with ExitStack() as s3:
                w2 = SB(s3, "w2", [128, 8, 1536], BF16)
                for (dst, src, n) in ((0, 0, 512), (512, 1536, 512), (1024, 2304, 256), (1280, 3072, 256)):
                    P.dma("pool", DMA(w2[:, :, dst:dst + n], wv[:, :, src:src + n]), writes=[f"w2_{dst}"])
                W2K = ["w2_0", "w2_512", "w2_1024", "w2_1280"]
                bt = SB(s3, "bt", [128, 5, 8, 128], BF16)
                bD = SB(s3, "bD", [128, 7, 8, 128], BF16)
                rm = SB(s3, "rm", [128, NB, 128], BF16)
                with ExitStack() as s3b:
                    bg = SB(s3b, "bg", [128, 7, 8, 128], F32)
                    cm = SB(s3b, "cm", [128, 7, 128], F32)
                    rmi = SB(s3b, "rmi", [128, 5, 128], F32)
                    P.dma("sp", DMA(bg[:].rearrange("p a h q -> p (a h q)"), biasg[l]), writes=["bg"])
                    P.dma("sp", DMA(cm[:].rearrange("p n q -> p (n q)"), cmd), writes=["cm"])
                    P.dma("sp", DMA(rmi[:].rearrange("p n q -> p (n q)"), rmintd), writes=["rmi"])
                    P.dma("pool", DMA(rm[:].rearrange("p n q -> p (n q)"), maskd), writes=["rm"])
                    for d in range(7):
                        for h in range(8):
                            P.op("dve", TT(bg[:, d, h, :], bg[:, d, h, :], cm[:, d, :], ALU.add),
                                 reads=["bg", "cm"], writes=[f"bgd{d}"])
                        P.op("act", ACT(bD[:, d, :, :], bg[:, d, :, :], AF.Copy), reads=[f"bgd{d}"], writes=["bD"])
                    for di, Dv in enumerate(INT_D):
                        d = (Dv + 6) // 2
                        for h in range(8):
                            P.op("dve", TT(bt[:, di, h, :], bg[:, d, h, :], rmi[:, di, :], ALU.add),
                                 reads=[f"bgd{d}", "rmi"], writes=["bt"])
                    P.barrier()
                dwf = SB(s3, "dwf", [128, 2, 31], F32)
                cv = SB(s3, "cv", [128, 3, 2], F32)
                dg = SB(s3, "dg", [128, 2, 31, 128], BF16)
                identf = SB(s3, "identf", [128, 128], F32)
                P.dma("sp", DMA(dwf[:], dww[l]), writes=["dwf"])
                P.dma("sp", DMA(cv[:], cvec[l]), writes=["cv"])
                P.dma("sp", DMA(identf[:], identd), writes=["identf"])
                for ch in range(2):
                    for j in range(31):
                        if j % 2 == 0:
                            P.op("dve", TS(dg[:, ch, j, :], identf[:], dwf[:, ch, j:j + 1], ALU.mult),
                                 reads=["identf", "dwf"], writes=["dgdve"])
                        else:
                            P.op("act", ACT(dg[:, ch, j, :], identf[:], AF.Copy, scale=dwf[:, ch, j:j + 1]),
                                 reads=["identf", "dwf"], writes=["dgact"])
                hTs = [SB(s3, f"hT{i}", [128, 8, 512], BF16) for i in range(2)]
                kws = [SB(s3, f"kw{i}", [128, 4, 1024], BF16) for i in range(2)]
                vws = [SB(s3, f"vw{i}", [128, 8, 520], BF16) for i in range(2)]
                hcw = [SB(s3, f"hcw{i}", [128, 2, 542], BF16) for i in range(2)]
                ybw = [SB(s3, f"ybw{i}", [128, 2, 512], BF16) for i in range(2)]
                ybp = [SB(s3, f"ybp{i}", [128, 2, 512], BF16) for i in range(2)]
                qTm = SB(s3, "qTm", [128, 4, 4, 2, 128], BF16)
                tht = SB(s3, "tht", [128, 512], F32)
                hg = SB(s3, "hg", [128, 512], F32)
                cvh = SB(s3, "cvh", [128, 3, 2], F32)
                P.op("dve", TS(cvh[:], cv[:], 0.5, ALU.mult), reads=["cv"], writes=["cvh"])
                g2 = SB(s3, "g2", [128, 8, 512], BF16)
                PT = SB(s3, "PT", [128, 6, 8, 128], BF16)
                gat = SB(s3, "gat", [128, 4, 512], BF16)
                ytok = [SB(s3, f"ytok{i}", [128, 512], BF16) for i in range(2)]
                rinv = SB(s3, "rinv", [128, 8], F32)
                ycs = [SB(s3, f"ycs{i}", [128, 8, 512], BF16) for i in range(2)]
                vt = SB(s3, "vt", [128, 2, 512], F32)
                v2 = SB(s3, "v2", [128, 2, 512], F32)
                msb = SB(s3, "msb", [128, 512], F32)
                var = SB(s3, "var", [128, 512], F32)
                t1 = SB(s3, "t1", [128, 512], F32)
                lnb = SB(s3, "lnb", [128, 512], F32)
                acc = [PS(s3, f"pa{i}", [128, 512], F32) for i in range(2)]
                Sb = [PS(s3, f"Sb{i}", [128, 512], F32) for i in range(3)]
                Ot = [PS(s3, f"Ot{i}", [128, 512], F32) for i in range(2)]
                Tp = PS(s3, "Tp", [128, 1024], BF16)
                na = [0]
                nsb = [0]
                nyt = [0]
                pending = [[]]
                pend_tr = [None]
                P.op("pool", MS(qTm[:], 0.0), writes=["qTm"])

                def tile_geo(g):
                    if g < NTA:
                        return "A", g, RA, 0
                    gi = g - NTA
                    return "B", gi % TPS, RP, LA + (gi // TPS) * LP

                def load2(g):
                    b = g % 2
                    unit, i, R, ts0 = tile_geo(g)
                    L = R * 64
                    t0 = g * 512
                    P.dma("sp", DMA(hTs[b][:], hTd[:, :, t0:t0 + 512].rearrange("j p n -> p j n")), writes=[f"hT{b}"])
                    lo_w = max(0, 8 * i - 4)
                    hi_w = min(R, 8 * i + 12)
                    nk = (hi_w - lo_w) * 64
                    P.dma("sp", DMA(kws[b][:, :, 0:nk], kTd[:, :, ts0 + lo_w * 64:ts0 + hi_w * 64].rearrange("j p n -> p j n")),
                          writes=[f"kw{b}"])
                    P.dma("sp", DMA(vws[b][:, 0:nk // 128, :],
                                    vd[ts0 + lo_w * 64:ts0 + hi_w * 64, :].rearrange("(c p) d -> p c d", p=128)),
                          writes=[f"vw{b}"])
                    a0 = max(ts0, t0 - 15)
                    a1 = min(ts0 + L, t0 + 512 + 15)
                    if a0 > t0 - 15:
                        P.op("pool", MS(hcw[b][:, :, 0:15], 0.0), writes=[f"hcw{b}"])
                    if a1 < t0 + 512 + 15:
                        P.op("pool", MS(hcw[b][:, :, 527:542], 0.0), writes=[f"hcw{b}"])
                    P.dma("sp", DMA(hcw[b][:, :, a0 - (t0 - 15):a1 - (t0 - 15)], hcd[:, :, a0:a1].rearrange("c p n -> p c n")),
                          writes=[f"hcw{b}"])
                    if unit == "A":
                        if i % TPS == 0 and i > 0:
                            P.op("pool", TS(hcw[b][:, :, 0:15], hcw[b][:, :, 0:15], flg[:, 0:1], ALU.mult),
                                 reads=["flg", f"hcw{b}"], writes=[f"hcw{b}"])
                        if (i + 1) % TPS == 0 and i + 1 < NTA:
                            P.op("pool", TS(hcw[b][:, :, 527:542], hcw[b][:, :, 527:542], flg[:, 0:1], ALU.mult),
                                 reads=["flg", f"hcw{b}"], writes=[f"hcw{b}"])
                    P.dma("sp", DMA(ybw[b][:], Ybd2[0][:, :, t0:t0 + 512].rearrange("c p n -> p c n")), writes=[f"ybw{b}"])
                    if unit == "A":
                        P.dma("sp", DMA(ybp[b][:], Ybd2[1][:, :, t0:t0 + 512].rearrange("c p n -> p c n")), writes=[f"ybp{b}"])

                load2(0)
                for g in range(NT):
                    b = g % 2
                    unit, i, R, ts0 = tile_geo(g)
                    t0 = g * 512
                    if g + 1 < NT:
                        load2(g + 1)
                    hT = hTs[b]
                    kw = kws[b]
                    vw = vws[b]
                    yc = ycs[b]
                    ycn = f"ycs{b}"
                    lo_w = max(0, 8 * i - 4)
                    def nxt_acc():
                        na[0] += 1
                        return acc[na[0] % 2], f"pa{na[0] % 2}"

                    for m in range(4):
                        a, an = nxt_acc()
                        P.group("pe", [MM(a[:], w2[:, j, m * 128:(m + 1) * 128], hT[:, j, :], j == 0, j == 7) for j in range(8)],
                                reads=["w2_0", f"hT{b}"], writes=[an])
                        av = a[:].rearrange("p (b q) -> p b q", b=4)
                        if m % 2 == 0:
                            P.op("act", ACT(qTm[0:64, m, :, 0, :], av[0:64], AF.Copy, scale=0.125), reads=[an], writes=["qTm"])
                            P.op("act", ACT(qTm[64:128, m, :, 1, :], av[64:128], AF.Copy, scale=0.125), reads=[an], writes=["qTm"])
                        else:
                            P.op("dve", TS(qTm[0:64, m, :, 0, :], av[0:64], 0.125, ALU.mult), reads=[an], writes=["qTm"])
                            P.op("dve", TS(qTm[64:128, m, :, 1, :], av[64:128], 0.125, ALU.mult), reads=[an], writes=["qTm"])

                    def f_gate(m):
                        a, an = nxt_acc()
                        P.group("pe", [MM(a[:], w2[:, j, 512 + m * 128:512 + (m + 1) * 128], hT[:, j, :], j == 0, j == 7)
                                       for j in range(8)], reads=["w2_1024" if m < 6 else "w2_1280", f"hT{b}"], writes=[an])
                        P.op("act", ACT(tht[:], a[:], AF.Tanh, scale=0.5), reads=[an], writes=["tht"])
                        P.op("act", ACT(hg[:], a[:], AF.Copy, scale=0.5), reads=[an], writes=["hg"])
                        P.op("dve", STT(g2[:, m, :], tht[:], 1.0, hg[:], ALU.add, ALU.mult), reads=["hg", "tht"], writes=[f"g2{m}"])

                    def f_yb():
                        for ch in range(2):
                            if unit == "A":
                                P.op("pool", TT(ybw[b][:, ch, :], ybw[b][:, ch, :], ybp[b][:, ch, :], ALU.add),
                                     reads=[f"ybw{b}", f"ybp{b}"], writes=[f"ybw{b}"])
                            P.op("pool", TT(yc[:, 4 + ch, :], ybw[b][:, ch, :], g2[:, 4 + ch, :], ALU.mult),
                                 reads=[f"ybw{b}", f"g2{4 + ch}"], writes=[ycn])

                    def f_conv(ch):
                        a, an = nxt_acc()
                        P.group("pe", [MM(a[:], dg[:, ch, j, :], hcw[b][:, ch, j:j + 512], j == 0, j == 30) for j in range(31)],
                                reads=["dgdve", "dgact", f"hcw{b}"], writes=[an])
                        P.op("act", ACT(vt[:, ch, :], a[:], AF.Identity, scale=0.5, bias=cv[:, 0, ch:ch + 1]),
                             reads=[an, "cv"], writes=[f"vt{ch}"])
                        P.op("act", ACT(v2[:, ch, :], a[:], AF.Square, scale=0.5, bias=cv[:, 0, ch:ch + 1]),
                             reads=[an, "cv"], writes=[f"v2{ch}"])

                    def f_ln_stats():
                        pm, pmn = nxt_acc()
                        P.group("pe", [MM(pm[:], onesf[:], vt[:, ch, :], ch == 0, ch == 1) for ch in range(2)],
                                reads=["onesf", "vt0", "vt1"], writes=[pmn])
                        pe2, pe2n = nxt_acc()
                        P.group("pe", [MM(pe2[:], onesf[:], v2[:, ch, :], ch == 0, ch == 1) for ch in range(2)],
                                reads=["onesf", "v20", "v21"], writes=[pe2n])
                        P.op("act", ACT(msb[:], pm[:], AF.Copy), reads=[pmn], writes=["msb"])
                        P.op("dve", TT(var[:], msb[:], msb[:], ALU.mult), reads=["msb"], writes=["var"])
                        P.op("dve", TT(var[:], pe2[:], var[:], ALU.subtract), reads=[pe2n, "var"], writes=["var"])

                    def ln_pieces(yc_, ycn_, t0_):
                        def p2():
                            P.op("act", ACT(var[:], var[:], AF.Sqrt, bias=EPS, scale=1.0), reads=["var"], writes=["var"])

                        def p3():
                            P.op("dve", RCP(var[:], var[:]), reads=["var"], writes=["var"])

                        def p4(ch):
                            def f():
                                P.op("dve", TT(t1[:], vt[:, ch, :], msb[:], ALU.subtract), reads=[f"vt{ch}", "msb"], writes=["t1"])
                                P.op("dve", TT(t1[:], t1[:], var[:], ALU.mult), reads=["t1", "var"], writes=["t1"])
                                P.op("act", ACT(lnb[:], t1[:], AF.Identity, scale=cvh[:, 1, ch:ch + 1], bias=cvh[:, 2, ch:ch + 1]),
                                     reads=["t1", "cvh"], writes=["lnb"])
                            return f

                        def p5(ch):
                            def f():
                                P.op("act", ACT(tht[:], lnb[:], AF.Tanh), reads=["lnb"], writes=["tht"])
                                P.op("dve", STT(t1[:], tht[:], 1.0, lnb[:], ALU.add, ALU.mult), reads=["tht", "lnb"], writes=["t1"])
                                P.op("pool", TT(yc_[:, 6 + ch, :], t1[:], g2[:, 6 + ch, :], ALU.mult),
                                     reads=["t1", f"g2{6 + ch}"], writes=[ycn_])
                            return f

                        def p6():
                            P.dma("sp", DMA(ycd[:, :, t0_:t0_ + 512].rearrange("j p n -> p j n"), yc_[:]), reads=[ycn_])
                        return [p2, p3, p4(0), p5(0), p4(1), p5(1), p6]

                    def f_gate_tok(sub):
                        a, an = nxt_acc()
                        P.group("pe", [MM(a[:], hT[:, j, sub * 128:(sub + 1) * 128], w2[:, j, 512:1024], j == 0, j == 7)
                                       for j in range(8)], reads=["w2_512", f"hT{b}"], writes=[an])
                        P.op("act", ACT(tht[:], a[:], AF.Tanh, scale=0.5), reads=[an], writes=["tht"])
                        P.op("act", ACT(hg[:], a[:], AF.Copy, scale=0.5), reads=[an], writes=["hg"])
                        P.op("dve", STT(gat[:, sub, :], tht[:], 1.0, hg[:], ALU.add, ALU.mult), reads=["hg", "tht"], writes=[f"gat{sub}"])

                    for m in range(4):
                        f_gate_tok(m)
                    fillers = pending[0] + [lambda: f_gate(4), lambda: (f_gate(5), f_yb()), lambda: f_conv(0), lambda: f_conv(1),
                                            lambda: f_gate(6), lambda: f_gate(7), f_ln_stats]
                    pending[0] = []

                    for mq in range(4):
                        r0 = 8 * i + 2 * mq
                        q0 = mq * 128
                        key, chunks, interior = qb_desc(unit, r0, RA, RP)
                        nch = len(chunks)
                        for ci, (c, Dv) in enumerate(chunks):
                            koff = (2 * c - lo_w) * 64
                            dD = (Dv + 6) // 2
                            for hh in range(2):
                                nsb[0] += 1
                                S = Sb[nsb[0] % 3]
                                Sn = f"Sb{nsb[0] % 3}"
                                fns = []
                                if interior:
                                    fns.append(MM(S[:], ident[:], bt[:, INT_D.index(Dv), 4 * hh:4 * hh + 4, :].rearrange("p h q -> p (h q)"), True, False))
                                else:
                                    fns.append(MM(S[:], ident[:], bD[:, dD, 4 * hh:4 * hh + 4, :].rearrange("p h q -> p (h q)"), True, False))
                                    for h4 in range(4):
                                        fns.append(MM(S[:, h4 * 128:(h4 + 1) * 128], ident[:],
                                                      rm[:, bidx[(key[0], key[1], Dv)], :], False, False))
                                for pl in range(2):
                                    j = 2 * hh + pl
                                    fns.append(MM(S[:, pl * 256:(pl + 1) * 256], kw[:, j, koff:koff + 128],
                                                  qTm[:, j, mq, :, :].rearrange("p a q -> p (a q)"), False, pl == 1))
                                P.group("pe", fns, reads=[f"kw{b}", "qTm", "bt", "bD", "rm", "ident"], writes=[Sn])
                                P.op("act", ACT(PT[:, ci, 4 * hh:4 * hh + 4, :], S[:].rearrange("p (h q) -> p h q", h=4), AF.Exp),
                                     reads=[Sn], writes=[f"PT{ci}"])
                            if ci == 2 and pend_tr[0] is not None:
                                pend_tr[0]()
                                pend_tr[0] = None
                            if fillers and ci % 2 == 1:
                                fillers.pop(0)()
                        if fillers:
                            fillers.pop(0)()
                        nyt[0] += 1
                        yt = ytok[nyt[0] % 2]
                        ytn = f"ytok{nyt[0] % 2}"
                        for hh in range(2):
                            fns = []
                            for h4 in range(4):
                                h = 4 * hh + h4
                                for ci, (c, Dv) in enumerate(chunks):
                                    fns.append(MM(Ot[hh][:, h4 * 65:(h4 + 1) * 65], PT[:, ci, h, :],
                                                  vw[:, (2 * c - lo_w) // 2, h * 65:(h + 1) * 65], ci == 0, ci == nch - 1))
                            P.group("pe", fns, reads=[f"PT{ci}" for ci in range(nch)] + [f"vw{b}"], writes=[f"Ot{hh}"])
                            otv = Ot[hh][:, 0:260].rearrange("p (h e) -> p h e", e=65)
                            P.op("dve", RCP(rinv[:, 4 * hh:4 * hh + 4], otv[:, :, 64]), reads=[f"Ot{hh}"], writes=[f"rinv{hh}"])
                            for h4 in range(4):
                                h = 4 * hh + h4
                                P.op("dve", STT(yt[:, h * 64:(h + 1) * 64], otv[:, h4, 0:64], rinv[:, h:h + 1],
                                                gat[:, mq, h * 64:(h + 1) * 64], ALU.mult, ALU.mult),
                                     reads=[f"Ot{hh}", f"rinv{hh}", f"gat{mq}"], writes=[ytn])
                            if hh == 0 and fillers:
                                fillers.pop(0)()
                        def mk_tr(yt_, ytn_, yc_, ycn_, q0_):
                            def f():
                                P.group("pe", [TR(Tp[:, j * 128:(j + 1) * 128], yt_[:, j * 128:(j + 1) * 128], ident[:]) for j in range(4)],
                                        reads=[ytn_, "ident"], writes=["Tp"])
                                P.op("act", ACT(yc_[:, 0:4, q0_:q0_ + 128], Tp[:, 0:512].rearrange("p (j q) -> p j q", j=4), AF.Copy),
                                     reads=["Tp"], writes=[ycn_])
                            return f
                        pend_tr[0] = mk_tr(yt, ytn, yc, ycn, q0)
                    while fillers:
                        fillers.pop(0)()
                    pending[0] = ln_pieces(yc, ycn, t0)
                    if g == NT - 1:
                        pend_tr[0]()
                        pend_tr[0] = None
                        for f_ in pending[0]:
                            f_()
                        pending[0] = []
            P.barrier()
            if stop == "p2":
                break
            with ExitStack() as s4:
                w3 = SB(s4, "w3", [128, 8, 3072], BF16)
                wb = SB(s4, "wb", [128, 8, D], BF16)
                wo = SB(s4, "wo", [128, 8, D], BF16)

                def ld_w3(n):
                    P.dma("pool", DMA(w3[:, :, n * 512:(n + 1) * 512], wv[:, :, 3328 + n * 512:3328 + (n + 1) * 512]), writes=[f"w3_{n}"])

                def ld_wb(n):
                    P.dma("pool", DMA(wb[:, :, n * 512:(n + 1) * 512],
                                      w_abc[l].rearrange("(j p) c -> p j c", p=128)[:, :, n * 512:(n + 1) * 512]), writes=[f"wb{n}"])

                def ld_wo(n):
                    P.dma("pool", DMA(wo[:, :, n * 512:(n + 1) * 512],
                                      w_o[l].rearrange("(j p) c -> p j c", p=128)[:, :, n * 512:(n + 1) * 512]), writes=[f"wo{n}"])
                ld_w3(0); ld_wb(0); ld_w3(2); ld_w3(4); ld_w3(1); ld_wb(1); ld_w3(3); ld_w3(5); ld_wo(0); ld_wo(1)
                pg = SB(s4, "pg", [128, D], F32)
                P.dma("sp", DMA(pg[:], post_g[l]), writes=["pg"])
                hTs = [SB(s4, f"hT{i}", [128, 8, 512], BF16) for i in range(2)]
                ycs = [SB(s4, f"ycs{i}", [128, 8, 512], BF16) for i in range(2)]
                xts = [SB(s4, f"xt{i}", [128, 4, D], F32) for i in range(2)]
                ths = [SB(s4, f"ths{i}", [128, 512], F32) for i in range(2)]
                tmp = [SB(s4, f"tmp{i}", [128, 512], F32) for i in range(2)]
                macc = SB(s4, "macc", [128, 512], F32)
                mg = SB(s4, "mg", [128, 8, 512], BF16)
                junk = SB(s4, "junk", [128, 512], F32)
                ss = SB(s4, "ss", [128, 2], F32)
                rs = SB(s4, "rs", [128, 1], F32)
                ot = [SB(s4, f"ot{i}", [128, D], F32) for i in range(2)]
                pgt = [PS(s4, f"pg{i}", [128, 512], F32) for i in range(2)]
                ppt = [PS(s4, f"pp{i}", [128, 512], F32) for i in range(2)]
                pot = [PS(s4, f"po{i}", [128, 512], F32) for i in range(4)]
                ng = [0]
                branches = ((0, (0, 1, 2, 3)), (1, (4, 5)), (2, (6, 7)))

                def load3(g):
                    b = g % 2
                    t0 = g * 512
                    P.dma("sp", DMA(hTs[b][:], hTd[:, :, t0:t0 + 512].rearrange("j p n -> p j n")), writes=[f"hT{b}"])
                    P.dma("sp", DMA(ycs[b][:], ycd[:, :, t0:t0 + 512].rearrange("j p n -> p j n")), writes=[f"ycs{b}"])
                    P.dma("sp", DMA(xts[b][:], xin[t0:t0 + 512, :].rearrange("(s p) d -> p s d", p=128)), writes=[f"xt{b}"])

                load3(0)
                nst = [0]
                for g in range(NT):
                    b = g % 2
                    t0 = g * 512
                    if g + 1 < NT:
                        load3(g + 1)
                    hT = hTs[b]
                    yc = ycs[b]
                    xt = xts[b]
                    for m in range(8):
                        for bi_, (br, kcs) in enumerate(branches):
                            ng[0] += 1
                            gi = ng[0] % 2
                            pgm = pgt[gi]
                            ppm = ppt[gi]
                            col = br * D + m * 128
                            P.group("pe", [MM(pgm[:], w3[:, j, col:col + 128], hT[:, j, :], j == 0, j == 7) for j in range(8)],
                                    reads=[f"w3_{col // 512}", f"hT{b}"], writes=[f"pg{gi}"])
                            P.op("act", ACT(ths[gi][:], pgm[:], AF.Tanh, scale=0.5), reads=[f"pg{gi}"], writes=[f"ths{gi}"])
                            P.group("pe", [MM(ppm[:], wb[:, kc, m * 128:(m + 1) * 128], yc[:, kc, :], kc == kcs[0], kc == kcs[-1])
                                           for kc in kcs], reads=[f"wb{m // 4}", f"ycs{b}"], writes=[f"pp{gi}"])
                            if bi_ == 0:
                                P.op("dve", STT(macc[:], ths[gi][:], 1.0, ppm[:], ALU.add, ALU.mult),
                                     reads=[f"ths{gi}", f"pp{gi}"], writes=["macc"])
                            else:
                                P.op("dve", STT(tmp[gi][:], ths[gi][:], 1.0, ppm[:], ALU.add, ALU.mult),
                                     reads=[f"ths{gi}", f"pp{gi}"], writes=[f"tmp{gi}"])
                                if bi_ == 1:
                                    P.op("pool", TT(macc[:], macc[:], tmp[gi][:], ALU.add), reads=[f"tmp{gi}", "macc"], writes=["macc"])
                                else:
                                    P.op("pool", TT(mg[:, m, :], macc[:], tmp[gi][:], ALU.add), reads=[f"tmp{gi}", "macc"],
                                         writes=[f"mg{m}"])
                    for s in range(4):
                        o = ot[s % 2]
                        on = f"ot{s % 2}"
                        pos = []
                        for hf in range(2):
                            nst[0] += 1
                            pi = nst[0] % 4
                            po_ = pot[pi]
                            pos.append((po_, f"po{pi}"))
                            P.group("pe", [MM(po_[:], mg[:, m, s * 128:(s + 1) * 128], wo[:, m, hf * 512:(hf + 1) * 512], m == 0, m == 7)
                                           for m in range(8)], reads=[f"wo{hf}"] + [f"mg{m}" for m in range(8)], writes=[f"po{pi}"])
                        P.op("dve", MS(ss[:], 0.0), writes=["ss"])
                        for hf in range(2):
                            P.op("act", ACT(junk[:], pos[hf][0][:], AF.Square, accum_out=ss[:, hf:hf + 1]),
                                 reads=[pos[hf][1]], writes=["junk", "ss"])
                        P.op("dve", TT(rs[:], ss[:, 0:1], ss[:, 1:2], ALU.add), reads=["ss"], writes=["rs"])
                        P.op("act", ACT(rs[:], rs[:], AF.Sqrt, bias=EPS, scale=1.0 / (4.0 * D)), reads=["rs"], writes=["rs"])
                        P.op("dve", RCP(rs[:], rs[:]), reads=["rs"], writes=["rs"])
                        P.op("dve", TS(rs[:], rs[:], 0.5, ALU.mult), reads=["rs"], writes=["rs"])
                        for hf in range(2):
                            P.op("dve", STT(o[:, hf * 512:(hf + 1) * 512], pos[hf][0][:], rs[:, 0:1], pg[:, hf * 512:(hf + 1) * 512],
                                            ALU.mult, ALU.mult), reads=[pos[hf][1], "rs", "pg"], writes=[on])
                        P.op("pool", TT(o[:], o[:], xt[:, s, :], ALU.add), reads=[on, f"xt{b}"], writes=[on])
                        P.dma("sp", DMA(xout[t0 + s * 128:t0 + (s + 1) * 128, :], o[:]), reads=[on])
            P.barrier()
        P.finish()
    return nc, bidx


def host_consts(LA, NSUB, NB_, bidx, inputs, depth=DEPTH):
    f = np.float32
    LP = LA // NSUB
    d = {}
    d["w_in"] = np.ascontiguousarray(inputs["w_in"][:depth], dtype=f)
    d["w_abc"] = np.ascontiguousarray(np.concatenate(
        [inputs["w_a_out"][:depth], inputs["w_b_out"][:depth], inputs["w_c_out"][:depth]], axis=1), dtype=f)
    d["w_o"] = np.ascontiguousarray(inputs["w_o"][:depth], dtype=f)
    d["pre_g"] = np.ascontiguousarray(inputs["pre_norm_g"][:depth].reshape(depth, 8, 128).transpose(0, 2, 1), dtype=f)
    d["post_g"] = np.ascontiguousarray(np.broadcast_to(inputs["post_norm_g"][:depth][:, None, :], (depth, 128, D)), dtype=f)
    d["dww"] = np.ascontiguousarray(inputs["c_dw_w"][:depth].reshape(depth, 31, 2, 128).transpose(0, 3, 2, 1), dtype=f)
    cv = np.stack([inputs["c_dw_b"][:depth], inputs["c_norm_g"][:depth], inputs["c_norm_b"][:depth]], axis=1)
    d["cvec"] = np.ascontiguousarray(cv.reshape(depth, 3, 2, 128).transpose(0, 3, 1, 2), dtype=f)
    d["biasg"] = np.ascontiguousarray(
        np.stack([gather_bias(np.asarray(inputs["rel_pos_bias"][i], dtype=f)) for i in range(depth)]).reshape(depth, 128, -1))
    d["rmintd"] = np.ascontiguousarray(make_rowmask_int().reshape(128, -1))
    d["cmd"] = np.ascontiguousarray(make_colmask().reshape(128, -1))
    d["identd"] = np.eye(128, dtype=f)
    C, S = chan_consts()
    d["c64d"] = np.ascontiguousarray(np.stack([C, S], axis=1))
    cfgs = {(1, LA), (NSUB, LP)}
    if NB_:
        cfgs.add((NB_, LP))
    for (G, L) in cfgs:
        d[f"w1d{G}_{L}"] = w1_blockdiag(2 * G, L)
    for L in {LA, LP}:
        A = L // 128
        _, _, m2 = dft_consts(L)
        d[f"m2d{L}"] = np.ascontiguousarray(m2.reshape(128, A * 2, 256))
    return d


def mode_consts(bidx, mode):
    m = {}
    m["maskd"] = np.ascontiguousarray(make_rowmask(bidx, mode).reshape(128, -1))
    fl = np.zeros((128, 2), np.float32)
    fl[:, 0 if mode == "s" else 1] = 1.0
    m["flagd"] = fl
    return m


_CACHE = {}


def kernel(**inputs):
    inputs = {k: np.asarray(v) for k, v in inputs.items()}
    xp = inputs["x_prompt"]
    xs = inputs["x_sample"]
    B, S, _ = xp.shape
    DB, DS, _ = xs.shape
    n = 8
    NSUB = DS // S
    NB_ = 2
    assert DB * NB_ + (n - DB) * (NSUB + NB_) == B
    key = (DS, NSUB, NB_)
    if key not in _CACHE:
        _CACHE[key] = build(DS, NSUB, NB_)
    nc, bidx = _CACHE[key]
    consts = host_consts(DS, NSUB, NB_, bidx, inputs)
    ms = mode_consts(bidx, "s")
    mp = mode_consts(bidx, "p")
    in_maps = []
    plan = []
    nxt = 0
    for c in range(n):
        if c < DB:
            pidx = list(range(nxt, nxt + NB_))
            nxt += NB_
            xc = np.concatenate([xs[c]] + [xp[i] for i in pidx], axis=0)
            plan.append(("s", c, pidx))
            m = dict(consts, **ms)
        else:
            pidx = list(range(nxt, nxt + NSUB + NB_))
            nxt += NSUB + NB_
            xc = np.concatenate([xp[i] for i in pidx], axis=0)
            plan.append(("p", None, pidx))
            m = dict(consts, **mp)
        m["x"] = np.ascontiguousarray(xc, dtype=np.float32)
        in_maps.append(m)
    res = run_bass_kernel_spmd(nc, in_maps, core_ids=list(range(n)))
    yp = np.empty_like(xp, dtype=np.float32)
    ys = np.empty_like(xs, dtype=np.float32)
    for c in range(n):
        y = res.results[c]["y"]
        mode, sidx, pidx = plan[c]
        off = 0
        if mode == "s":
            ys[sidx] = y[0:DS]
            off = DS
        for i in pidx:
            yp[i] = y[off:off + S]
            off += S
    return (yp, ys)
```

```python
import os
import numpy as np
from contextlib import ExitStack
import concourse.bass as bass
import concourse.mybir as mybir
from concourse.bass_utils import run_bass_kernel_spmd

F32 = mybir.dt.float32
BF16 = mybir.dt.bfloat16
ALU = mybir.AluOpType
AF = mybir.ActivationFunctionType

EPOCH = 30000
NDMA_SEM = 20
D = 1024
DIN = 6400
EPS = 1e-6
NEG = -30000.0
DEPTH = 2


class Prog:
    ENGS = ("pe", "act", "dve", "pool", "sp")

    def __init__(self, nc, stack):
        self.nc = nc
        self.stack = stack
        self.ops = {e: [] for e in self.ENGS}
        self.cnt = {e: 0 for e in self.ENGS}
        self.sems = {e: [] for e in self.ENGS}
        self.seen = {e: {} for e in self.ENGS}
        self.regions = {}
        self.dma_sems = {}
        self.dma_val = {}
        self.dma_rr = {"sp": 0, "pool": 0, "act": 0}
        self.nsem = {"sp": NDMA_SEM, "pool": 6}
        for q in ("sp", "pool"):
            self.dma_sems[q] = [self._mksem(f"d{q}{i}") for i in range(self.nsem[q])]

    def _mksem(self, name):
        s = self.stack.enter_context(self.nc.semaphore(name))
        self.dma_val[name] = 0
        return (name, s)

    def _eng_sem(self, e, n):
        idx = (n - 1) // EPOCH
        while len(self.sems[e]) <= idx:
            nm = f"s{e}{len(self.sems[e])}"
            self.sems[e].append((nm, self.stack.enter_context(self.nc.semaphore(nm))))
        return self.sems[e][idx], n - idx * EPOCH

    def _wait(self, eng, ticket):
        if ticket is None:
            return
        if ticket[0] == "eng":
            _, f, n = ticket
            (nm, s), v = self._eng_sem(f, n)
        else:
            _, (nm, s), v = ticket
        if self.seen[eng].get(nm, 0) >= v:
            return
        self.seen[eng][nm] = v
        self.ops[eng].append(lambda e, s=s, v=v: e.wait_ge(s, v))

    def _deps(self, eng, reads, writes):
        for k in reads:
            r = self.regions.get(k)
            if r is not None:
                for t in r["w"].values():
                    self._wait(eng, t)
        for k in writes:
            r = self.regions.get(k)
            if r is not None:
                for t in r["w"].values():
                    self._wait(eng, t)
                for t in r["r"].values():
                    self._wait(eng, t)

    def _record(self, ticket, reads, writes):
        rk = ticket[1] if ticket[0] == "eng" else ticket[1][0]
        for k in writes:
            r = self.regions.setdefault(k, {"w": {}, "r": {}})
            r["w"][rk] = ticket
            r["r"] = {}
        for k in reads:
            if k in writes:
                continue
            r = self.regions.setdefault(k, {"w": {}, "r": {}})
            r["r"][rk] = ticket

    def op(self, eng, fn, reads=(), writes=()):
        return self.group(eng, [fn], reads, writes)

    def group(self, eng, fns, reads=(), writes=()):
        self._deps(eng, reads, writes)
        for fn in fns[:-1]:
            self.ops[eng].append(lambda e, fn=fn: fn(e))
        self.cnt[eng] += 1
        n = self.cnt[eng]
        (nm, s), v = self._eng_sem(eng, n)
        fn = fns[-1]
        self.ops[eng].append(lambda e, fn=fn, s=s: fn(e).then_inc(s, 1))
        t = ("eng", eng, n)
        self._record(t, reads, writes)
        return t

    def dma(self, q, fn, reads=(), writes=()):
        self._deps(q, reads, writes)
        i = self.dma_rr[q]
        self.dma_rr[q] = (i + 1) % self.nsem[q]
        nm, s = self.dma_sems[q][i]
        prev = self.dma_val[nm]
        if prev > 0:
            self._wait(q, ("dma", (nm, s), prev))
        v = prev + 16
        self.dma_val[nm] = v
        self.ops[q].append(lambda e, fn=fn, s=s: fn(e).then_inc(s, 16))
        t = ("dma", (nm, s), v)
        self._record(t, reads, writes)
        return t

    def barrier(self):
        for e in self.ENGS:
            for f in ("pe", "act", "dve", "pool"):
                if self.cnt[f] > 0:
                    self._wait(e, ("eng", f, self.cnt[f]))
            for q in ("sp", "pool"):
                for nm, s in self.dma_sems[q]:
                    v = self.dma_val[nm]
                    if v > 0:
                        self._wait(e, ("dma", (nm, s), v))
        self.regions = {}

    def finish(self):
        self.barrier()
        nc = self.nc
        with nc.Block() as blk:
            @blk.tensor
            def _(e):
                for f in self.ops["pe"]:
                    f(e)

            @blk.scalar
            def _(e):
                for f in self.ops["act"]:
                    f(e)

            @blk.vector
            def _(e):
                for f in self.ops["dve"]:
                    f(e)

            @blk.gpsimd
            def _(e):
                for f in self.ops["pool"]:
                    f(e)

            @blk.sync
            def _(e):
                for f in self.ops["sp"]:
                    f(e)


def MM(out, lhsT, rhs, start, stop):
    return lambda e: e.matmul(out, lhsT=lhsT, rhs=rhs, start=start, stop=stop)


def TR(out, in_, ident):
    return lambda e: e.transpose(out, in_, ident)


def ACT(out, in_, func, **kw):
    return lambda e: e.activation(out=out, in_=in_, func=func, **kw)


def TT(out, in0, in1, op):
    return lambda e: e.tensor_tensor(out=out, in0=in0, in1=in1, op=op)


def TS(out, in0, s1, op0):
    return lambda e: e.tensor_scalar(out=out, in0=in0, scalar1=s1, scalar2=None, op0=op0)


def STT(out, in0, scalar, in1, op0, op1):
    return lambda e: e.scalar_tensor_tensor(out=out, in0=in0, scalar=scalar, in1=in1, op0=op0, op1=op1)


def CP(out, in_):
    return lambda e: e.tensor_copy(out=out, in_=in_)


def RCP(out, in_):
    return lambda e: e.reciprocal(out=out, in_=in_)


def MS(ap, v):
    return lambda e: e.memset(ap, v)


def DMA(out, in_):
    return lambda e: e.dma_start(out=out, in_=in_)


def r_start(r, R):
    return min(max(r - 4, 0), R - 8)


def qblock_info(r0, R):
    s0 = r_start(r0, R) - r0
    s1 = r_start(r0 + 1, R) - r0
    lo = r_start(r0, R)
    hi = r_start(r0 + 1, R) + 8
    chunks = list(range(lo // 2, (hi + 1) // 2))
    return (s0, s1), [(c, 2 * c - r0) for c in chunks]


def unitA_qblock(r0, RA, RP):
    ts_, chs = qblock_info(r0, RA)
    base = (r0 // RP) * RP
    tp_, chp = qblock_info(r0 - base, RP)
    ds = {Dv for _, Dv in chs} | {Dv for _, Dv in chp}
    return (ts_, tp_), [((r0 + Dv) // 2, Dv) for Dv in sorted(ds)]


INT_T = (-4, -3)
INT_D = (-4, -2, 0, 2, 4)


def qb_desc(unit, r0, RA, RP):
    if unit == "A":
        key, ch = unitA_qblock(r0, RA, RP)
    else:
        t, ch = qblock_info(r0, RP)
        key = (t, t)
    interior = key == (INT_T, INT_T) and tuple(d for _, d in ch) == INT_D
    return key, ch, interior


def enum_rowmask(LA, NSUB, NB):
    idx = {}
    RA = LA // 64
    RP = RA // NSUB
    for r0 in range(0, RA, 2):
        key, ch, interior = qb_desc("A", r0, RA, RP)
        if not interior:
            for c, Dv in ch:
                idx.setdefault((key[0], key[1], Dv), len(idx))
    if NB:
        for r0 in range(0, RP, 2):
            key, ch, interior = qb_desc("B", r0, RA, RP)
            if not interior:
                for c, Dv in ch:
                    idx.setdefault((key[0], key[1], Dv), len(idx))
    return idx


def rowmask_block(typ, Dv):
    m = np.full((128, 128), NEG, np.float32)
    for krp in range(2):
        for qrp in range(2):
            s = typ[qrp]
            if s <= Dv + krp < s + 8:
                m[krp * 64:(krp + 1) * 64, qrp * 64:(qrp + 1) * 64] = 0.0
    return m


def make_rowmask(idx, mode):
    m = np.zeros((128, max(len(idx), 1), 128), np.float32)
    for (ts_, tp_, Dv), i in idx.items():
        m[:, i, :] = rowmask_block(ts_ if mode == "s" else tp_, Dv)
    return m


def make_rowmask_int():
    return np.stack([rowmask_block(INT_T, Dv) for Dv in INT_D], axis=1)


def make_colmask():
    col = np.arange(64)
    cs = np.clip(col - 8, 0, 48)
    colv = (col[:, None] >= cs[None, :]) & (col[:, None] < cs[None, :] + 16)
    m = np.full((128, 7, 128), NEG, np.float32)
    for d in range(7):
        Dv = 2 * d - 6
        for krp in range(2):
            for qrp in range(2):
                if abs(Dv + krp - qrp) <= 7:
                    m[krp * 64:(krp + 1) * 64, d, qrp * 64:(qrp + 1) * 64] = np.where(colv, 0.0, NEG)
    return m


def gather_bias(rpb):
    col = np.arange(64)
    dc = np.clip(col[:, None] - col[None, :] + 15, 0, 30)
    out = np.zeros((128, 7, 8, 128), np.float32)
    for d in range(7):
        Dv = 2 * d - 6
        for krp in range(2):
            for qrp in range(2):
                dr = int(np.clip(Dv + krp - qrp + 7, 0, 14))
                blk = rpb[:, dr][:, dc]
                out[krp * 64:(krp + 1) * 64, d, :, qrp * 64:(qrp + 1) * 64] = blk.transpose(1, 0, 2)
    return out


def dft_consts(L):
    A = L // 128
    s = 1.0 / np.sqrt(64.0 * L)
    a = np.arange(A)
    ang1 = 2 * np.pi * np.outer(a, a) / A
    w1c = (np.cos(ang1) * s).astype(np.float32)
    w1s = (-np.sin(ang1) * s).astype(np.float32)
    b = np.arange(128, dtype=np.float64)
    ka = np.arange(A, dtype=np.float64)
    kb = np.arange(128, dtype=np.float64)
    k = ka[:, None] + A * kb[None, :]
    ang2 = 2 * np.pi * b[:, None, None] * k[None] / L
    mr = np.cos(ang2)
    mi = -np.sin(ang2)
    m2 = np.zeros((128, A, 2, 256), np.float32)
    m2[:, :, 0, 0:128] = mr
    m2[:, :, 0, 128:256] = mi
    m2[:, :, 1, 0:128] = -mi
    m2[:, :, 1, 128:256] = mr
    return w1c, w1s, m2


def w1_blockdiag(G, L):
    A = L // 128
    w1c, w1s, _ = dft_consts(L)
    K = G * A
    out = np.zeros((K, 2, K), np.float32)
    for g in range(G):
        out[g * A:(g + 1) * A, 0, g * A:(g + 1) * A] = w1c
        out[g * A:(g + 1) * A, 1, g * A:(g + 1) * A] = w1s
    return out


def chan_consts():
    c = np.arange(64)
    ang = 2 * np.pi * np.outer(c, c) / 64
    C = np.zeros((128, 128), np.float32)
    S = np.zeros((128, 128), np.float32)
    for g in range(2):
        C[g * 64:(g + 1) * 64, g * 64:(g + 1) * 64] = np.cos(ang)
        S[g * 64:(g + 1) * 64, g * 64:(g + 1) * 64] = np.sin(ang)
    return C, S


def build(LA, NSUB, NB_, depth=DEPTH, stop=None):
    LP = LA // NSUB
    T = LA + NB_ * LP
    NT = T // 512
    RA = LA // 64
    RP = LP // 64
    NTA = LA // 512
    TPS = LP // 512
    bidx = enum_rowmask(LA, NSUB, NB_)
    NB = max(len(bidx), 1)
    Ls = sorted({LA, LP})
    groups = [(0, 1, LA, 0, 0), (0, NSUB, LP, 1, 1)]
    if NB_:
        groups.append((LA, NB_, LP, 0, None))
    gcfg = sorted({(G, L) for _, G, L, _, _ in groups})

    nc = bass.Bass("TRN2", target_bir_lowering=False)
    dram = lambda n, sh, dt, kind="Internal": nc.dram_tensor(n, sh, dt, kind=kind).ap()
    x_in = dram("x", [T, D], F32, "ExternalInput")
    y_out = dram("y", [T, D], F32, "ExternalOutput")
    w_in = dram("w_in", [depth, D, DIN], F32, "ExternalInput")
    w_abc = dram("w_abc", [depth, D, D], F32, "ExternalInput")
    w_o = dram("w_o", [depth, D, D], F32, "ExternalInput")
    pre_g = dram("pre_g", [depth, 128, 8], F32, "ExternalInput")
    post_g = dram("post_g", [depth, 128, D], F32, "ExternalInput")
    dww = dram("dww", [depth, 128, 2, 31], F32, "ExternalInput")
    cvec = dram("cvec", [depth, 128, 3, 2], F32, "ExternalInput")
    biasg = dram("biasg", [depth, 128, 7 * 8 * 128], F32, "ExternalInput")
    rmintd = dram("rmintd", [128, 5 * 128], F32, "ExternalInput")
    cmd = dram("cmd", [128, 7 * 128], F32, "ExternalInput")
    maskd = dram("maskd", [128, NB * 128], F32, "ExternalInput")
    identd = dram("identd", [128, 128], F32, "ExternalInput")
    c64d = dram("c64d", [128, 2, 128], F32, "ExternalInput")
    w1d = {(G, L): dram(f"w1d{G}_{L}", [2 * G * L // 128, 2, 2 * G * L // 128], F32, "ExternalInput") for (G, L) in gcfg}
    flagd = dram("flagd", [128, 2], F32, "ExternalInput")
    m2d = {L: dram(f"m2d{L}", [128, (L // 128) * 2, 256], F32, "ExternalInput") for L in Ls}
    xmid = dram("xmid", [T, D], F32)
    m2b = {L: dram(f"m2b{L}", [128, (L // 128) * 2, 256], BF16) for L in Ls}
    hTd = dram("hTd", [8, 128, T], BF16)
    kTd = dram("kTd", [4, 128, T], BF16)
    vd = dram("vd", [T, 520], BF16)
    Ud = dram("Ud", [T, 256], BF16)
    Zdg = [dram(f"Zd{i}", [2, G_ * L_ * 256], BF16) for i, (_, G_, L_, _, _) in enumerate(groups)]
    Ybd2 = [dram(f"Ybd{i}", [2, 128, T], BF16) for i in range(2)]
    hcd = dram("hcd", [2, 128, T], BF16)
    ycd = dram("ycd", [8, 128, T], BF16)

    with ExitStack() as st:
        P = Prog(nc, st)

        uid = [0]

        def SB(stk, n, sh, dt):
            uid[0] += 1
            return stk.enter_context(nc.sbuf_tensor(f"{n}_{uid[0]}", sh, dt))

        def PS(stk, n, sh, dt):
            uid[0] += 1
            return stk.enter_context(nc.psum_tensor(f"{n}_{uid[0]}", sh, dt))

        ident = SB(st, "ident", [128, 128], BF16)
        onesb = SB(st, "onesb", [128, 128], BF16)
        onesf = SB(st, "onesf", [128, 128], F32)
        c64 = SB(st, "c64", [128, 2, 128], BF16)
        P.dma("pool", DMA(ident[:], identd), writes=["ident"])
        P.dma("pool", DMA(c64[:], c64d), writes=["c64"])
        flg = SB(st, "flg", [128, 2], F32)
        P.dma("sp", DMA(flg[:], flagd), writes=["flg"])
        P.op("dve", MS(onesb[:], 1.0), writes=["onesb"])
        P.op("dve", MS(onesf[:], 1.0 / 256.0), writes=["onesf"])

        for l in range(depth):
            xin = x_in if l == 0 else xmid
            xout = y_out if l == depth - 1 else xmid
            if depth == 1:
                xout = y_out
            wv = w_in[l].rearrange("(j p) c -> p j c", p=128)
            P.barrier()
            with ExitStack() as s1:
                w1 = SB(s1, "w1", [128, 8, 1792], BF16)
                for (dst, src, n) in ((0, 512, 512), (512, 1024, 512), (1024, 2048, 256),
                                      (1280, 2560, 256), (1536, 2816, 256)):
                    P.dma("pool", DMA(w1[:, :, dst:dst + n], wv[:, :, src:src + n]), writes=[f"w1_{dst}"])
                W1K = ["w1_0", "w1_512", "w1_1024", "w1_1280", "w1_1536"]
                if l == 0:
                    cst = [SB(s1, f"cst{i}", [128, 8, 256], BF16) for i in range(2)]
                    kk = 0
                    for L_ in Ls:
                        for h2 in range(0, 2 * (L_ // 128), 8):
                            P.dma("pool", DMA(cst[kk % 2][:], m2d[L_][:, h2:h2 + 8, :]), writes=[f"cst{kk % 2}"])
                            P.dma("pool", DMA(m2b[L_][:, h2:h2 + 8, :], cst[kk % 2][:]), reads=[f"cst{kk % 2}"])
                            kk += 1
                g1 = SB(s1, "g1", [128, 8], F32)
                P.dma("sp", DMA(g1[:], pre_g[l]), writes=["g1"])
                xts = [SB(s1, f"xt{i}", [128, 4, D], F32) for i in range(2)]
                hb = SB(s1, "hb", [128, 4, D], BF16)
                junk = SB(s1, "junk", [128, D], F32)
                ss = SB(s1, "ss", [128, 4], F32)
                rs = SB(s1, "rs", [128, 4], F32)
                hTs = [SB(s1, f"hT{i}", [128, 8, 512], BF16) for i in range(2)]
                kTs = SB(s1, "kTs", [128, 4, 512], BF16)
                vts = SB(s1, "vts", [128, 4, 520], BF16)
                uts = SB(s1, "uts", [128, 4, 256], BF16)
                th = SB(s1, "th", [128, 2, 512], F32)
                hcs = SB(s1, "hcs", [128, 2, 512], BF16)
                tps = [PS(s1, f"tp{i}", [128, 1024], BF16) for i in range(2)]
                acc = [PS(s1, f"acc{i}", [128, 512], F32) for i in range(6)]
                na = [0]

                def nacc():
                    na[0] += 1
                    i = na[0] % 6
                    return acc[i], f"acc{i}"

                def load_x(g):
                    b = g % 2
                    P.dma("sp", DMA(xts[b][:], xin[g * 512:(g + 1) * 512, :].rearrange("(s p) d -> p s d", p=128)),
                          writes=[f"xt{b}"])

                P.op("pool", MS(vts[:], 1.0), writes=["vts"])
                hbs = [hb, SB(s1, "hb2", [128, 4, D], BF16)]

                def norm(g):
                    b = g % 2
                    xt = xts[b]
                    hbb = hbs[b]
                    P.op("dve", MS(ss[:], 0.0), writes=["ss"])
                    for s in range(4):
                        P.op("act", ACT(junk[:], xt[:, s, :], AF.Square, accum_out=ss[:, s:s + 1]),
                             reads=[f"xt{b}"], writes=["junk", "ss"])
                    P.op("act", ACT(rs[:], ss[:], AF.Sqrt, bias=EPS, scale=1.0 / D), reads=["ss"], writes=["rs"])
                    P.op("dve", RCP(rs[:], rs[:]), reads=["rs"], writes=["rs"])
                    for s in range(4):
                        if s % 2 == 0:
                            P.op("act", ACT(hbb[:, s, :], xt[:, s, :], AF.Copy, scale=rs[:, s:s + 1]),
                                 reads=[f"xt{b}", "rs"], writes=[f"hb{b}_{s}"])
                        else:
                            P.op("dve", TS(hbb[:, s, :], xt[:, s, :], rs[:, s:s + 1], ALU.mult),
                                 reads=[f"xt{b}", "rs"], writes=[f"hb{b}_{s}"])

                load_x(0)
                if NT > 1:
                    load_x(1)
                norm(0)
                for g in range(NT):
                    b = g % 2
                    t0 = g * 512
                    hT = hTs[b]
                    hbb = hbs[b]
                    for j in range(8):
                        tp = tps[j % 2]
                        P.group("pe", [TR(tp[:, s * 128:(s + 1) * 128], hbb[:, s, j * 128:(j + 1) * 128], ident[:])
                                       for s in range(4)],
                                reads=[f"hb{b}_{s}" for s in range(4)] + ["ident"], writes=[f"tp{j % 2}"])
                        if j % 2 == 0:
                            P.op("dve", TS(hT[:, j, :], tp[:, 0:512], g1[:, j:j + 1], ALU.mult),
                                 reads=[f"tp{j % 2}", "g1"], writes=[f"hT{b}"])
                        else:
                            P.op("act", ACT(hT[:, j, :], tp[:, 0:512], AF.Copy, scale=g1[:, j:j + 1]),
                                 reads=[f"tp{j % 2}", "g1"], writes=[f"hT{b}"])
                    if g + 1 < NT:
                        norm(g + 1)
                    if g + 2 < NT:
                        load_x(g + 2)
                    P.dma("sp", DMA(hTd[:, :, t0:t0 + 512].rearrange("j p n -> p j n"), hT[:]), reads=[f"hT{b}"])
                    for m in range(4):
                        a, an = nacc()
                        P.group("pe", [MM(a[:], w1[:, j, m * 128:(m + 1) * 128], hT[:, j, :], j == 0, j == 7)
                                       for j in range(8)], reads=["w1_0", f"hT{b}"], writes=[an])
                        if m % 2 == 0:
                            P.op("act", ACT(kTs[:, m, :], a[:], AF.Copy), reads=[an], writes=["kTs"])
                        else:
                            P.op("dve", CP(kTs[:, m, :], a[:]), reads=[an], writes=["kTs"])
                    P.dma("sp", DMA(kTd[:, :, t0:t0 + 512].rearrange("j p n -> p j n"), kTs[:]), reads=["kTs"])
                    for s in range(4):
                        a, an = nacc()
                        P.group("pe", [MM(a[:], hT[:, j, s * 128:(s + 1) * 128], w1[:, j, 512:1024], j == 0, j == 7)
                                       for j in range(8)], reads=["w1_512", f"hT{b}"], writes=[an])
                        vdst = vts[:, s, :].rearrange("p (h e) -> p h e", e=65)[:, :, 0:64]
                        vsrc = a[:].rearrange("p (h e) -> p h e", e=64)
                        if s % 2 == 0:
                            P.op("dve", CP(vdst, vsrc), reads=[an], writes=["vts"])
                        else:
                            P.op("act", ACT(vdst, vsrc, AF.Copy), reads=[an], writes=["vts"])
                    P.dma("sp", DMA(vd[t0:t0 + 512, :].rearrange("(s p) d -> p s d", p=128), vts[:]), reads=["vts"])
                    for s in range(4):
                        a, an = nacc()
                        P.group("pe", [MM(a[:, 0:256], hT[:, j, s * 128:(s + 1) * 128], w1[:, j, 1024:1280], j == 0, j == 7)
                                       for j in range(8)], reads=["w1_1024", f"hT{b}"], writes=[an])
                        P.op("dve", CP(uts[:, s, :], a[:, 0:256]), reads=[an], writes=["uts"])
                    P.dma("sp", DMA(Ud[t0:t0 + 512, :].rearrange("(s p) d -> p s d", p=128), uts[:]), reads=["uts"])
                    for ch in range(2):
                        a, an = nacc()
                        P.group("pe", [MM(a[:], w1[:, j, 1536 + ch * 128:1536 + (ch + 1) * 128], hT[:, j, :], j == 0, j == 7)
                                       for j in range(8)], reads=["w1_1536", f"hT{b}"], writes=[an])
                        P.op("act", ACT(th[:, ch, :], a[:], AF.Tanh, scale=0.5), reads=[an], writes=[f"th{ch}"])
                        a2, an2 = nacc()
                        P.group("pe", [MM(a2[:], w1[:, j, 1280 + ch * 128:1280 + (ch + 1) * 128], hT[:, j, :], j == 0, j == 7)
                                       for j in range(8)], reads=["w1_1280", f"hT{b}"], writes=[an2])
                        P.op("dve", STT(hcs[:, ch, :], th[:, ch, :], 1.0, a2[:], ALU.add, ALU.mult),
                             reads=[an2, f"th{ch}"], writes=["hcs"])
                    P.dma("sp", DMA(hcd[:, :, t0:t0 + 512].rearrange("c p n -> p c n"), hcs[:]), reads=["hcs"])
            P.barrier()
            if stop == "p1":
                break
            KMAX = max(G * L // 128 for _, G, L, _, _ in groups)
            TGMAX = max(G * L for _, G, L, _, _ in groups)
            with ExitStack() as s2:
                usb = SB(s2, "usb", [2 * KMAX, 16384], BF16)
                zst = [SB(s2, f"zst{i}", [2 * KMAX, 2, 4096], BF16) for i in range(2)]
                zp = [PS(s2, f"zp{i}", [128, 512], F32) for i in range(4)]
                for gi_, (ts0, G, L, obuf, fcol) in enumerate(groups):
                    A = L // 128
                    K = G * A
                    K2 = 2 * K
                    TG = G * L
                    w1c = SB(s2, f"w1c{gi_}", [K2, 2, K2], BF16)
                    P.dma("pool", DMA(w1c[:], w1d[(G, L)]), writes=[f"w1c{gi_}"])
                    udv = Ud[ts0:ts0 + TG, :].rearrange("(a b) c -> a (b c)", b=128)
                    for gq in range(4):
                        for hb in range(2):
                            P.dma("sp", DMA(usb[hb * K:(hb + 1) * K, gq * 4096:(gq + 1) * 4096],
                                            udv[:, hb * 16384 + gq * 4096:hb * 16384 + (gq + 1) * 4096]), writes=[f"usb{gq}_{hb}"])
                    for gq in range(4):
                        zs = zst[gq % 2]
                        for n8 in range(8):
                            n = gq * 8 + n8
                            for r in range(2):
                                pi = (n8 % 2) * 2 + r
                                P.group("pe", [MM(zp[pi][0:K2, :], w1c[:, r, :], usb[0:K2, n * 512:(n + 1) * 512], True, True)],
                                        reads=[f"w1c{gi_}", f"usb{gq}_0", f"usb{gq}_1"], writes=[f"zp{pi}"])
                                if r == 0:
                                    P.op("act", ACT(zs[0:K2, r, n8 * 512:(n8 + 1) * 512], zp[pi][0:K2, :], AF.Copy),
                                         reads=[f"zp{pi}"], writes=[f"zst{gq % 2}"])
                                else:
                                    P.op("dve", CP(zs[0:K2, r, n8 * 512:(n8 + 1) * 512], zp[pi][0:K2, :]),
                                         reads=[f"zp{pi}"], writes=[f"zst{gq % 2}"])
                        for r in range(2):
                            zdv = Zdg[gi_][r, :].rearrange("(a n) -> a n", a=K)
                            for hb in range(2):
                                P.dma("sp", DMA(zdv[:, hb * 16384 + gq * 4096:hb * 16384 + (gq + 1) * 4096], zs[hb * K:(hb + 1) * K, r, :]),
                                      reads=[f"zst{gq % 2}"], writes=[f"Zd{gi_}_{r}_{hb}"])
            P.barrier()
            with ExitStack() as s2:
                AS = LA // 128
                AP_ = LP // 128
                m2s = SB(s2, "m2s", [128, AS * 2, 256], BF16)
                m2p = SB(s2, "m2p", [128, AP_ * 2, 256], BF16)
                zb = [SB(s2, f"zb{r}", [128, KMAX, 256], BF16) for r in range(2)]
                xr = SB(s2, "xr", [128, TGMAX], BF16)
                xi = SB(s2, "xi", [128, TGMAX], BF16)
                yst = SB(s2, "yst", [128, TGMAX], BF16)
                xp = [PS(s2, f"xp{i}", [128, 512], F32) for i in range(3)]
                yp = [PS(s2, f"yp{i}", [128, 512], F32) for i in range(2)]
                for (mt, Lm, nm) in ((m2s, LA, "m2s"), (m2p, LP, "m2p")):
                    for h2 in range(0, 2 * (Lm // 128), 32):
                        h3 = min(2 * (Lm // 128), h2 + 32)
                        P.dma("sp", DMA(mt[:, h2:h3, :], m2b[Lm][:, h2:h3, :]), writes=[f"{nm}_{h2}"])
                for gi_, (ts0, G, L, obuf, fcol) in enumerate(groups):
                    A = L // 128
                    K = G * A
                    TG = G * L
                    m2, m2n = (m2s, "m2s") if (L == LA and gi_ == 0) else (m2p, "m2p")
                    for a0_ in range(0, K, 16):
                        a1_ = min(K, a0_ + 16)
                        for r in range(2):
                            zv = Zdg[gi_][r, :].rearrange("(a b c) -> b a c", a=K, b=128, c=256)
                            P.dma("sp", DMA(zb[r][:, a0_:a1_, :], zv[:, a0_:a1_, :]), writes=[f"zb{r}_{a0_}"])
                    for cc in range(2):
                        for idx_ in range(K):
                            ka = idx_ % A
                            xpp = xp[idx_ % 3]
                            P.group("pe", [MM(xpp[:, 0:256], zb[0][:, idx_, cc * 128:(cc + 1) * 128], m2[:, ka * 2 + 0, :], True, False),
                                           MM(xpp[:, 0:256], zb[1][:, idx_, cc * 128:(cc + 1) * 128], m2[:, ka * 2 + 1, :], False, True)],
                                    reads=[f"zb0_{(idx_ // 16) * 16}", f"zb1_{(idx_ // 16) * 16}", f"{m2n}_{((ka * 2) // 32) * 32}"],
                                    writes=[f"xp{idx_ % 3}"])
                            xrv = xr[:, idx_ * 128:(idx_ + 1) * 128]
                            xiv = xi[:, idx_ * 128:(idx_ + 1) * 128]
                            if idx_ % 2 == 0:
                                P.op("act", ACT(xrv, xpp[:, 0:128], AF.Copy), reads=[f"xp{idx_ % 3}"], writes=["xr"])
                                P.op("act", ACT(xiv, xpp[:, 128:256], AF.Copy), reads=[f"xp{idx_ % 3}"], writes=["xi"])
                            else:
                                P.op("dve", CP(xrv, xpp[:, 0:128]), reads=[f"xp{idx_ % 3}"], writes=["xr"])
                                P.op("dve", CP(xiv, xpp[:, 128:256]), reads=[f"xp{idx_ % 3}"], writes=["xi"])
                        nkb = 512 // A
                        cnt_ = 0
                        for g_ in range(G):
                            for n in range(L // 512):
                                ypp = yp[cnt_ % 2]
                                ypn = f"yp{cnt_ % 2}"
                                xrs = xr[:, g_ * L:(g_ + 1) * L].rearrange("p (a kb) -> p kb a", a=A)[:, n * nkb:(n + 1) * nkb, :]
                                xis = xi[:, g_ * L:(g_ + 1) * L].rearrange("p (a kb) -> p kb a", a=A)[:, n * nkb:(n + 1) * nkb, :]
                                P.group("pe", [MM(ypp[:].rearrange("p (kb a) -> p kb a", a=A), c64[:, 0, :], xrs, True, False),
                                               MM(ypp[:].rearrange("p (kb a) -> p kb a", a=A), c64[:, 1, :], xis, False, True)],
                                        reads=["c64", "xr", "xi"], writes=[ypn])
                                yo = yst[:, g_ * L + n * 512:g_ * L + (n + 1) * 512]
                                if fcol is None:
                                    if cnt_ % 2 == 0:
                                        P.op("act", ACT(yo, ypp[:], AF.Copy), reads=[ypn], writes=["yst"])
                                    else:
                                        P.op("dve", CP(yo, ypp[:]), reads=[ypn], writes=["yst"])
                                else:
                                    if cnt_ % 2 == 0:
                                        P.op("act", ACT(yo, ypp[:], AF.Copy, scale=flg[:, fcol:fcol + 1]), reads=[ypn, "flg"], writes=["yst"])
                                    else:
                                        P.op("dve", TS(yo, ypp[:], flg[:, fcol:fcol + 1], ALU.mult), reads=[ypn, "flg"], writes=["yst"])
                                cnt_ += 1
                        P.dma("sp", DMA(Ybd2[obuf][cc, :, ts0:ts0 + TG], yst[:, 0:TG]), reads=["yst"])
            P.barrier()
            if stop == "fnet":
                break
            with ExitStack() as s3:
                w2 = SB(s3, "w2", [128, 8, 1536], BF16)
                for (dst, src, n) in ((0, 0, 512), (512, 1536, 512), (1024, 2304, 256), (1280, 3072, 256)):
                    P.dma("pool", DMA(w2[:, :, dst:dst + n], wv[:, :, src:src + n]), writes=[f"w2_{dst}"])
                W2K = ["w2_0", "w2_512", "w2_1024", "w2_1280"]
                bt = SB(s3, "bt", [128, 5, 8, 128], BF16)
                bD = SB(s3, "bD", [128, 7, 8, 128], BF16)
                rm = SB(s3, "rm", [128, NB, 128], BF16)
                with ExitStack() as s3b:
                    bg = SB(s3b, "bg", [128, 7, 8, 128], F32)
                    cm = SB(s3b, "cm", [128, 7, 128], F32)
                    rmi = SB(s3b, "rmi", [128, 5, 128], F32)
                    P.dma("sp", DMA(bg[:].rearrange("p a h q -> p (a h q)"), biasg[l]), writes=["bg"])
                    P.dma("sp", DMA(cm[:].rearrange("p n q -> p (n q)"), cmd), writes=["cm"])
                    P.dma("sp", DMA(rmi[:].rearrange("p n q -> p (n q)"), rmintd), writes=["rmi"])
                    P.dma("pool", DMA(rm[:].rearrange("p n q -> p (n q)"), maskd), writes=["rm"])
                    for d in range(7):
                        for h in range(8):
                            P.op("dve", TT(bg[:, d, h, :], bg[:, d, h, :], cm[:, d, :], ALU.add),
                                 reads=["bg", "cm"], writes=[f"bgd{d}"])
                        P.op("act", ACT(bD[:, d, :, :], bg[:, d, :, :], AF.Copy), reads=[f"bgd{d}"], writes=["bD"])
                    for di, Dv in enumerate(INT_D):
                        d = (Dv + 6) // 2
                        for h in range(8):
                            P.op("dve", TT(bt[:, di, h, :], bg[:, d, h, :], rmi[:, di, :], ALU.add),
                                 reads=[f"bgd{d}", "rmi"], writes=["bt"])
                    P.barrier()
                dwf = SB(s3, "dwf", [128, 2, 31], F32)
                cv = SB(s3, "cv", [128, 3, 2], F32)
                dg = SB(s3, "dg", [128, 2, 31, 128], BF16)
                identf = SB(s3, "identf", [128, 128], F32)
                P.dma("sp", DMA(dwf[:], dww[l]), writes=["dwf"])
                P.dma("sp", DMA(cv[:], cvec[l]), writes=["cv"])
                P.dma("sp", DMA(identf[:], identd), writes=["identf"])
                for ch in range(2):
                    for j in range(31):
                        if j % 2 == 0:
                            P.op("dve", TS(dg[:, ch, j, :], identf[:], dwf[:, ch, j:j + 1], ALU.mult),
                                 reads=["identf", "dwf"], writes=["dgdve"])
                        else:
                            P.op("act", ACT(dg[:, ch, j, :], identf[:], AF.Copy, scale=dwf[:, ch, j:j + 1]),
                                 reads=["identf", "dwf"], writes=["dgact"])
                hTs = [SB(s3, f"hT{i}", [128, 8, 512], BF16) for i in range(2)]
                kws = [SB(s3, f"kw{i}", [128, 4, 1024], BF16) for i in range(2)]
                vws = [SB(s3, f"vw{i}", [128, 8, 520], BF16) for i in range(2)]
                hcw = [SB(s3, f"hcw{i}", [128, 2, 542], BF16) for i in range(2)]
                ybw = [SB(s3, f"ybw{i}", [128, 2, 512], BF16) for i in range(2)]
                ybp = [SB(s3, f"ybp{i}", [128, 2, 512], BF16) for i in range(2)]
                qTm = SB(s3, "qTm", [128, 4, 4, 2, 128], BF16)
                tht = SB(s3, "tht", [128, 512], F32)
                hg = SB(s3, "hg", [128, 512], F32)
                cvh = SB(s3, "cvh", [128, 3, 2], F32)
                P.op("dve", TS(cvh[:], cv[:], 0.5, ALU.mult), reads=["cv"], writes=["cvh"])
                g2 = SB(s3, "g2", [128, 8, 512], BF16)
                PT = SB(s3, "PT", [128, 6, 8, 128], BF16)
                gat = SB(s3, "gat", [128, 4, 512], BF16)
                ytok = [SB(s3, f"ytok{i}", [128, 512], BF16) for i in range(2)]
                rinv = SB(s3, "rinv", [128, 8], F32)
                ycs = [SB(s3, f"ycs{i}", [128, 8, 512], BF16) for i in range(2)]
                vt = SB(s3, "vt", [128, 2, 512], F32)
                v2 = SB(s3, "v2", [128, 2, 512], F32)
                msb = SB(s3, "msb", [128, 512], F32)
                var = SB(s3, "var", [128, 512], F32)
                t1 = SB(s3, "t1", [128, 512], F32)
                lnb = SB(s3, "lnb", [128, 512], F32)
                acc = [PS(s3, f"pa{i}", [128, 512], F32) for i in range(2)]
                Sb = [PS(s3, f"Sb{i}", [128, 512], F32) for i in range(3)]
                Ot = [PS(s3, f"Ot{i}", [128, 512], F32) for i in range(2)]
                Tp = PS(s3, "Tp", [128, 1024], BF16)
                na = [0]
                nsb = [0]
                nyt = [0]
                pending = [[]]
                pend_tr = [None]
                P.op("pool", MS(qTm[:], 0.0), writes=["qTm"])

                def tile_geo(g):
                    if g < NTA:
                        return "A", g, RA, 0
                    gi = g - NTA
                    return "B", gi % TPS, RP, LA + (gi // TPS) * LP

                def load2(g):
                    b = g % 2
                    unit, i, R, ts0 = tile_geo(g)
                    L = R * 64
                    t0 = g * 512
                    P.dma("sp", DMA(hTs[b][:], hTd[:, :, t0:t0 + 512].rearrange("j p n -> p j n")), writes=[f"hT{b}"])
                    lo_w = max(0, 8 * i - 4)
                    hi_w = min(R, 8 * i + 12)
                    nk = (hi_w - lo_w) * 64
                    P.dma("sp", DMA(kws[b][:, :, 0:nk], kTd[:, :, ts0 + lo_w * 64:ts0 + hi_w * 64].rearrange("j p n -> p j n")),
                          writes=[f"kw{b}"])
                    P.dma("sp", DMA(vws[b][:, 0:nk // 128, :],
                                    vd[ts0 + lo_w * 64:ts0 + hi_w * 64, :].rearrange("(c p) d -> p c d", p=128)),
                          writes=[f"vw{b}"])
                    a0 = max(ts0, t0 - 15)
                    a1 = min(ts0 + L, t0 + 512 + 15)
                    if a0 > t0 - 15:
                        P.op("pool", MS(hcw[b][:, :, 0:15], 0.0), writes=[f"hcw{b}"])
                    if a1 < t0 + 512 + 15:
                        P.op("pool", MS(hcw[b][:, :, 527:542], 0.0), writes=[f"hcw{b}"])
                    P.dma("sp", DMA(hcw[b][:, :, a0 - (t0 - 15):a1 - (t0 - 15)], hcd[:, :, a0:a1].rearrange("c p n -> p c n")),
                          writes=[f"hcw{b}"])
                    if unit == "A":
                        if i % TPS == 0 and i > 0:
                            P.op("pool", TS(hcw[b][:, :, 0:15], hcw[b][:, :, 0:15], flg[:, 0:1], ALU.mult),
                                 reads=["flg", f"hcw{b}"], writes=[f"hcw{b}"])
                        if (i + 1) % TPS == 0 and i + 1 < NTA:
                            P.op("pool", TS(hcw[b][:, :, 527:542], hcw[b][:, :, 527:542], flg[:, 0:1], ALU.mult),
                                 reads=["flg", f"hcw{b}"], writes=[f"hcw{b}"])
                    P.dma("sp", DMA(ybw[b][:], Ybd2[0][:, :, t0:t0 + 512].rearrange("c p n -> p c n")), writes=[f"ybw{b}"])
                    if unit == "A":
                        P.dma("sp", DMA(ybp[b][:], Ybd2[1][:, :, t0:t0 + 512].rearrange("c p n -> p c n")), writes=[f"ybp{b}"])

                load2(0)
                for g in range(NT):
                    b = g % 2
                    unit, i, R, ts0 = tile_geo(g)
                    t0 = g * 512
                    if g + 1 < NT:
                        load2(g + 1)
                    hT = hTs[b]
                    kw = kws[b]
                    vw = vws[b]
                    yc = ycs[b]
                    ycn = f"ycs{b}"
                    lo_w = max(0, 8 * i - 4)
                    def nxt_acc():
                        na[0] += 1
                        return acc[na[0] % 2], f"pa{na[0] % 2}"

                    for m in range(4):
                        a, an = nxt_acc()
                        P.group("pe", [MM(a[:], w2[:, j, m * 128:(m + 1) * 128], hT[:, j, :], j == 0, j == 7) for j in range(8)],
                                reads=["w2_0", f"hT{b}"], writes=[an])
                        av = a[:].rearrange("p (b q) -> p b q", b=4)
                        if False:
                            pass
                        else:
                            P.op("dve", TS(qTm[0:64, m, :, 0, :], av[0:64], 0.125, ALU.mult), reads=[an], writes=["qTm"])
                            P.op("dve", TS(qTm[64:128, m, :, 1, :], av[64:128], 0.125, ALU.mult), reads=[an], writes=["qTm"])

                    def f_gate(m):
                        a, an = nxt_acc()
                        P.group("pe", [MM(a[:], w2[:, j, 512 + m * 128:512 + (m + 1) * 128], hT[:, j, :], j == 0, j == 7)
                                       for j in range(8)], reads=["w2_1024" if m < 6 else "w2_1280", f"hT{b}"], writes=[an])
                        P.op("act", ACT(tht[:], a[:], AF.Tanh, scale=0.5), reads=[an], writes=["tht"])
                        P.op("act", ACT(hg[:], a[:], AF.Copy, scale=0.5), reads=[an], writes=["hg"])
                        P.op("dve", STT(g2[:, m, :], tht[:], 1.0, hg[:], ALU.add, ALU.mult), reads=["hg", "tht"], writes=[f"g2{m}"])

                    def f_yb():
                        for ch in range(2):
                            if unit == "A":
                                P.op("pool", TT(ybw[b][:, ch, :], ybw[b][:, ch, :], ybp[b][:, ch, :], ALU.add),
                                     reads=[f"ybw{b}", f"ybp{b}"], writes=[f"ybw{b}"])
                            P.op("pool", TT(yc[:, 4 + ch, :], ybw[b][:, ch, :], g2[:, 4 + ch, :], ALU.mult),
                                 reads=[f"ybw{b}", f"g2{4 + ch}"], writes=[ycn])

                    def f_conv(ch):
                        a, an = nxt_acc()
                        P.group("pe", [MM(a[:], dg[:, ch, j, :], hcw[b][:, ch, j:j + 512], j == 0, j == 30) for j in range(31)],
                                reads=["dgdve", "dgact", f"hcw{b}"], writes=[an])
                        P.op("act", ACT(vt[:, ch, :], a[:], AF.Identity, scale=0.5, bias=cv[:, 0, ch:ch + 1]),
                             reads=[an, "cv"], writes=[f"vt{ch}"])
                        P.op("act", ACT(v2[:, ch, :], a[:], AF.Square, scale=0.5, bias=cv[:, 0, ch:ch + 1]),
                             reads=[an, "cv"], writes=[f"v2{ch}"])

                    def f_ln_stats():
                        pm, pmn = nxt_acc()
                        P.group("pe", [MM(pm[:], onesf[:], vt[:, ch, :], ch == 0, ch == 1) for ch in range(2)],
                                reads=["onesf", "vt0", "vt1"], writes=[pmn])
                        pe2, pe2n = nxt_acc()
                        P.group("pe", [MM(pe2[:], onesf[:], v2[:, ch, :], ch == 0, ch == 1) for ch in range(2)],
                                reads=["onesf", "v20", "v21"], writes=[pe2n])
                        P.op("act", ACT(msb[:], pm[:], AF.Copy), reads=[pmn], writes=["msb"])
                        P.op("dve", TT(var[:], msb[:], msb[:], ALU.mult), reads=["msb"], writes=["var"])
                        P.op("dve", TT(var[:], pe2[:], var[:], ALU.subtract), reads=[pe2n, "var"], writes=["var"])

                    def ln_pieces(yc_, ycn_, t0_):
                        def p2():
                            P.op("act", ACT(var[:], var[:], AF.Sqrt, bias=EPS, scale=1.0), reads=["var"], writes=["var"])

                        def p3():
                            P.op("dve", RCP(var[:], var[:]), reads=["var"], writes=["var"])

                        def p4(ch):
                            def f():
                                P.op("dve", TT(t1[:], vt[:, ch, :], msb[:], ALU.subtract), reads=[f"vt{ch}", "msb"], writes=["t1"])
                                P.op("dve", TT(t1[:], t1[:], var[:], ALU.mult), reads=["t1", "var"], writes=["t1"])
                                P.op("act", ACT(lnb[:], t1[:], AF.Identity, scale=cvh[:, 1, ch:ch + 1], bias=cvh[:, 2, ch:ch + 1]),
                                     reads=["t1", "cvh"], writes=["lnb"])
                            return f

                        def p5(ch):
                            def f():
                                P.op("act", ACT(tht[:], lnb[:], AF.Tanh), reads=["lnb"], writes=["tht"])
                                P.op("dve", STT(t1[:], tht[:], 1.0, lnb[:], ALU.add, ALU.mult), reads=["tht", "lnb"], writes=["t1"])
                                P.op("pool", TT(yc_[:, 6 + ch, :], t1[:], g2[:, 6 + ch, :], ALU.mult),
                                     reads=["t1", f"g2{6 + ch}"], writes=[ycn_])
                            return f

                        def p6():
                            P.dma("sp", DMA(ycd[:, :, t0_:t0_ + 512].rearrange("j p n -> p j n"), yc_[:]), reads=[ycn_])
                        return [p2, p3, p4(0), p5(0), p4(1), p5(1), p6]

                    def f_gate_tok(sub):
                        a, an = nxt_acc()
                        P.group("pe", [MM(a[:], hT[:, j, sub * 128:(sub + 1) * 128], w2[:, j, 512:1024], j == 0, j == 7)
                                       for j in range(8)], reads=["w2_512", f"hT{b}"], writes=[an])
                        P.op("act", ACT(tht[:], a[:], AF.Tanh, scale=0.5), reads=[an], writes=["tht"])
                        P.op("act", ACT(hg[:], a[:], AF.Copy, scale=0.5), reads=[an], writes=["hg"])
                        P.op("dve", STT(gat[:, sub, :], tht[:], 1.0, hg[:], ALU.add, ALU.mult), reads=["hg", "tht"], writes=[f"gat{sub}"])

                    for m in range(4):
                        f_gate_tok(m)
                    fillers = pending[0] + [lambda: f_gate(4), lambda: (f_gate(5), f_yb()), lambda: f_conv(0), lambda: f_conv(1),
                                            lambda: f_gate(6), lambda: f_gate(7), f_ln_stats]
                    pending[0] = []

                    for mq in range(4):
                        r0 = 8 * i + 2 * mq
                        q0 = mq * 128
                        key, chunks, interior = qb_desc(unit, r0, RA, RP)
                        nch = len(chunks)
                        for ci, (c, Dv) in enumerate(chunks):
                            koff = (2 * c - lo_w) * 64
                            dD = (Dv + 6) // 2
                            for hh in range(2):
                                nsb[0] += 1
                                S = Sb[nsb[0] % 3]
                                Sn = f"Sb{nsb[0] % 3}"
                                fns = []
                                if interior:
                                    fns.append(MM(S[:], ident[:], bt[:, INT_D.index(Dv), 4 * hh:4 * hh + 4, :].rearrange("p h q -> p (h q)"), True, False))
                                else:
                                    fns.append(MM(S[:], ident[:], bD[:, dD, 4 * hh:4 * hh + 4, :].rearrange("p h q -> p (h q)"), True, False))
                                    for h4 in range(4):
                                        fns.append(MM(S[:, h4 * 128:(h4 + 1) * 128], ident[:],
                                                      rm[:, bidx[(key[0], key[1], Dv)], :], False, False))
                                for pl in range(2):
                                    j = 2 * hh + pl
                                    fns.append(MM(S[:, pl * 256:(pl + 1) * 256], kw[:, j, koff:koff + 128],
                                                  qTm[:, j, mq, :, :].rearrange("p a q -> p (a q)"), False, pl == 1))
                                P.group("pe", fns, reads=[f"kw{b}", "qTm", "bt", "bD", "rm", "ident"], writes=[Sn])
                                P.op("act", ACT(PT[:, ci, 4 * hh:4 * hh + 4, :], S[:].rearrange("p (h q) -> p h q", h=4), AF.Exp),
                                     reads=[Sn], writes=[f"PT{ci}"])
                            if ci == 2 and pend_tr[0] is not None:
                                pend_tr[0]()
                                pend_tr[0] = None
                            if fillers and ci % 2 == 1:
                                fillers.pop(0)()
                        if fillers:
                            fillers.pop(0)()
                        nyt[0] += 1
                        yt = ytok[nyt[0] % 2]
                        ytn = f"ytok{nyt[0] % 2}"
                        for hh in range(2):
                            fns = []
                            for h4 in range(4):
                                h = 4 * hh + h4
                                for ci, (c, Dv) in enumerate(chunks):
                                    fns.append(MM(Ot[hh][:, h4 * 65:(h4 + 1) * 65], PT[:, ci, h, :],
                                                  vw[:, (2 * c - lo_w) // 2, h * 65:(h + 1) * 65], ci == 0, ci == nch - 1))
                            P.group("pe", fns, reads=[f"PT{ci}" for ci in range(nch)] + [f"vw{b}"], writes=[f"Ot{hh}"])
                            otv = Ot[hh][:, 0:260].rearrange("p (h e) -> p h e", e=65)
                            P.op("dve", RCP(rinv[:, 4 * hh:4 * hh + 4], otv[:, :, 64]), reads=[f"Ot{hh}"], writes=[f"rinv{hh}"])
                            for h4 in range(4):
                                h = 4 * hh + h4
                                P.op("dve", STT(yt[:, h * 64:(h + 1) * 64], otv[:, h4, 0:64], rinv[:, h:h + 1],
                                                gat[:, mq, h * 64:(h + 1) * 64], ALU.mult, ALU.mult),
                                     reads=[f"Ot{hh}", f"rinv{hh}", f"gat{mq}"], writes=[ytn])
                            if hh == 0 and fillers:
                                fillers.pop(0)()
                        def mk_tr(yt_, ytn_, yc_, ycn_, q0_):
                            def f():
                                P.group("pe", [TR(Tp[:, j * 128:(j + 1) * 128], yt_[:, j * 128:(j + 1) * 128], ident[:]) for j in range(4)],
                                        reads=[ytn_, "ident"], writes=["Tp"])
                                P.op("dve", CP(yc_[:, 0:4, q0_:q0_ + 128], Tp[:, 0:512].rearrange("p (j q) -> p j q", j=4)),
                                     reads=["Tp"], writes=[ycn_])
                            return f
                        pend_tr[0] = mk_tr(yt, ytn, yc, ycn, q0)
                    while fillers:
                        fillers.pop(0)()
                    pending[0] = ln_pieces(yc, ycn, t0)
                    if g == NT - 1:
                        pend_tr[0]()
                        pend_tr[0] = None
                        for f_ in pending[0]:
                            f_()
                        pending[0] = []
            P.barrier()
            if stop == "p2":
                break
            with ExitStack() as s4:
                w3 = SB(s4, "w3", [128, 8, 3072], BF16)
                wb = SB(s4, "wb", [128, 8, D], BF16)
                wo = SB(s4, "wo", [128, 8, D], BF16)

                def ld_w3(n):
                    P.dma("pool", DMA(w3[:, :, n * 512:(n + 1) * 512], wv[:, :, 3328 + n * 512:3328 + (n + 1) * 512]), writes=[f"w3_{n}"])

                def ld_wb(n):
                    P.dma("pool", DMA(wb[:, :, n * 512:(n + 1) * 512],
                                      w_abc[l].rearrange("(j p) c -> p j c", p=128)[:, :, n * 512:(n + 1) * 512]), writes=[f"wb{n}"])

                def ld_wo(n):
                    P.dma("pool", DMA(wo[:, :, n * 512:(n + 1) * 512],
                                      w_o[l].rearrange("(j p) c -> p j c", p=128)[:, :, n * 512:(n + 1) * 512]), writes=[f"wo{n}"])
                ld_w3(0); ld_wb(0); ld_w3(2); ld_w3(4); ld_w3(1); ld_wb(1); ld_w3(3); ld_w3(5); ld_wo(0); ld_wo(1)
                pg = SB(s4, "pg", [128, D], F32)
                P.dma("sp", DMA(pg[:], post_g[l]), writes=["pg"])
                hTs = [SB(s4, f"hT{i}", [128, 8, 512], BF16) for i in range(2)]
                ycs = [SB(s4, f"ycs{i}", [128, 8, 512], BF16) for i in range(2)]
                xts = [SB(s4, f"xt{i}", [128, 4, D], F32) for i in range(2)]
                ths = [SB(s4, f"ths{i}", [128, 512], F32) for i in range(2)]
                tmp = [SB(s4, f"tmp{i}", [128, 512], F32) for i in range(2)]
                macc = SB(s4, "macc", [128, 512], F32)
                mg = SB(s4, "mg", [128, 8, 512], BF16)
                junk = SB(s4, "junk", [128, 512], F32)
                ss = SB(s4, "ss", [128, 2], F32)
                rs = SB(s4, "rs", [128, 1], F32)
                ot = [SB(s4, f"ot{i}", [128, D], F32) for i in range(2)]
                pgt = [PS(s4, f"pg{i}", [128, 512], F32) for i in range(2)]
                ppt = [PS(s4, f"pp{i}", [128, 512], F32) for i in range(2)]
                pot = [PS(s4, f"po{i}", [128, 512], F32) for i in range(4)]
                ng = [0]
                branches = ((0, (0, 1, 2, 3)), (1, (4, 5)), (2, (6, 7)))

                def load3(g):
                    b = g % 2
                    t0 = g * 512
                    P.dma("sp", DMA(hTs[b][:], hTd[:, :, t0:t0 + 512].rearrange("j p n -> p j n")), writes=[f"hT{b}"])
                    P.dma("sp", DMA(ycs[b][:], ycd[:, :, t0:t0 + 512].rearrange("j p n -> p j n")), writes=[f"ycs{b}"])
                    P.dma("sp", DMA(xts[b][:], xin[t0:t0 + 512, :].rearrange("(s p) d -> p s d", p=128)), writes=[f"xt{b}"])

                load3(0)
                nst = [0]
                for g in range(NT):
                    b = g % 2
                    t0 = g * 512
                    if g + 1 < NT:
                        load3(g + 1)
                    hT = hTs[b]
                    yc = ycs[b]
                    xt = xts[b]
                    for m in range(8):
                        for bi_, (br, kcs) in enumerate(branches):
                            ng[0] += 1
                            gi = ng[0] % 2
                            pgm = pgt[gi]
                            ppm = ppt[gi]
                            col = br * D + m * 128
                            P.group("pe", [MM(pgm[:], w3[:, j, col:col + 128], hT[:, j, :], j == 0, j == 7) for j in range(8)],
                                    reads=[f"w3_{col // 512}", f"hT{b}"], writes=[f"pg{gi}"])
                            P.op("act", ACT(ths[gi][:], pgm[:], AF.Tanh, scale=0.5), reads=[f"pg{gi}"], writes=[f"ths{gi}"])
                            P.group("pe", [MM(ppm[:], wb[:, kc, m * 128:(m + 1) * 128], yc[:, kc, :], kc == kcs[0], kc == kcs[-1])
                                           for kc in kcs], reads=[f"wb{m // 4}", f"ycs{b}"], writes=[f"pp{gi}"])
                            if bi_ == 0:
                                P.op("dve", STT(macc[:], ths[gi][:], 1.0, ppm[:], ALU.add, ALU.mult),
                                     reads=[f"ths{gi}", f"pp{gi}"], writes=["macc"])
                            else:
                                P.op("dve", STT(tmp[gi][:], ths[gi][:], 1.0, ppm[:], ALU.add, ALU.mult),
                                     reads=[f"ths{gi}", f"pp{gi}"], writes=[f"tmp{gi}"])
                                if bi_ == 1:
                                    P.op("pool", TT(macc[:], macc[:], tmp[gi][:], ALU.add), reads=[f"tmp{gi}", "macc"], writes=["macc"])
                                else:
                                    P.op("pool", TT(mg[:, m, :], macc[:], tmp[gi][:], ALU.add), reads=[f"tmp{gi}", "macc"],
                                         writes=[f"mg{m}"])
                    for s in range(4):
                        o = ot[s % 2]
                        on = f"ot{s % 2}"
                        pos = []
                        for hf in range(2):
                            nst[0] += 1
                            pi = nst[0] % 4
                            po_ = pot[pi]
                            pos.append((po_, f"po{pi}"))
                            P.group("pe", [MM(po_[:], mg[:, m, s * 128:(s + 1) * 128], wo[:, m, hf * 512:(hf + 1) * 512], m == 0, m == 7)
                                           for m in range(8)], reads=[f"wo{hf}"] + [f"mg{m}" for m in range(8)], writes=[f"po{pi}"])
                        P.op("dve", MS(ss[:], 0.0), writes=["ss"])
                        for hf in range(2):
                            P.op("act", ACT(junk[:], pos[hf][0][:], AF.Square, accum_out=ss[:, hf:hf + 1]),
                                 reads=[pos[hf][1]], writes=["junk", "ss"])
                        P.op("dve", TT(rs[:], ss[:, 0:1], ss[:, 1:2], ALU.add), reads=["ss"], writes=["rs"])
                        P.op("act", ACT(rs[:], rs[:], AF.Sqrt, bias=EPS, scale=1.0 / (4.0 * D)), reads=["rs"], writes=["rs"])
                        P.op("dve", RCP(rs[:], rs[:]), reads=["rs"], writes=["rs"])
                        P.op("dve", TS(rs[:], rs[:], 0.5, ALU.mult), reads=["rs"], writes=["rs"])
                        for hf in range(2):
                            P.op("dve", STT(o[:, hf * 512:(hf + 1) * 512], pos[hf][0][:], rs[:, 0:1], pg[:, hf * 512:(hf + 1) * 512],
                                            ALU.mult, ALU.mult), reads=[pos[hf][1], "rs", "pg"], writes=[on])
                        P.op("pool", TT(o[:], o[:], xt[:, s, :], ALU.add), reads=[on, f"xt{b}"], writes=[on])
                        P.dma("sp", DMA(xout[t0 + s * 128:t0 + (s + 1) * 128, :], o[:]), reads=[on])
            P.barrier()
        P.finish()
    return nc, bidx


def host_consts(LA, NSUB, NB_, bidx, inputs, depth=DEPTH):
    f = np.float32
    LP = LA // NSUB
    d = {}
    d["w_in"] = np.ascontiguousarray(inputs["w_in"][:depth], dtype=f)
    d["w_abc"] = np.ascontiguousarray(np.concatenate(
        [inputs["w_a_out"][:depth], inputs["w_b_out"][:depth], inputs["w_c_out"][:depth]], axis=1), dtype=f)
    d["w_o"] = np.ascontiguousarray(inputs["w_o"][:depth], dtype=f)
    d["pre_g"] = np.ascontiguousarray(inputs["pre_norm_g"][:depth].reshape(depth, 8, 128).transpose(0, 2, 1), dtype=f)
    d["post_g"] = np.ascontiguousarray(np.broadcast_to(inputs["post_norm_g"][:depth][:, None, :], (depth, 128, D)), dtype=f)
    d["dww"] = np.ascontiguousarray(inputs["c_dw_w"][:depth].reshape(depth, 31, 2, 128).transpose(0, 3, 2, 1), dtype=f)
    cv = np.stack([inputs["c_dw_b"][:depth], inputs["c_norm_g"][:depth], inputs["c_norm_b"][:depth]], axis=1)
    d["cvec"] = np.ascontiguousarray(cv.reshape(depth, 3, 2, 128).transpose(0, 3, 1, 2), dtype=f)
    d["biasg"] = np.ascontiguousarray(
        np.stack([gather_bias(np.asarray(inputs["rel_pos_bias"][i], dtype=f)) for i in range(depth)]).reshape(depth, 128, -1))
    d["rmintd"] = np.ascontiguousarray(make_rowmask_int().reshape(128, -1))
    d["cmd"] = np.ascontiguousarray(make_colmask().reshape(128, -1))
    d["identd"] = np.eye(128, dtype=f)
    C, S = chan_consts()
    d["c64d"] = np.ascontiguousarray(np.stack([C, S], axis=1))
    cfgs = {(1, LA), (NSUB, LP)}
    if NB_:
        cfgs.add((NB_, LP))
    for (G, L) in cfgs:
        d[f"w1d{G}_{L}"] = w1_blockdiag(2 * G, L)
    for L in {LA, LP}:
        A = L // 128
        _, _, m2 = dft_consts(L)
        d[f"m2d{L}"] = np.ascontiguousarray(m2.reshape(128, A * 2, 256))
    return d


def mode_consts(bidx, mode):
    m = {}
    m["maskd"] = np.ascontiguousarray(make_rowmask(bidx, mode).reshape(128, -1))
    fl = np.zeros((128, 2), np.float32)
    fl[:, 0 if mode == "s" else 1] = 1.0
    m["flagd"] = fl
    return m


_CACHE = {}


def kernel(**inputs):
    inputs = {k: np.asarray(v) for k, v in inputs.items()}
    xp = inputs["x_prompt"]
    xs = inputs["x_sample"]
    B, S, _ = xp.shape
    DB, DS, _ = xs.shape
    n = 8
    NSUB = DS // S
    NB_ = 2
    assert DB * NB_ + (n - DB) * (NSUB + NB_) == B
    key = (DS, NSUB, NB_)
    if key not in _CACHE:
        _CACHE[key] = build(DS, NSUB, NB_)
    nc, bidx = _CACHE[key]
    consts = host_consts(DS, NSUB, NB_, bidx, inputs)
    ms = mode_consts(bidx, "s")
    mp = mode_consts(bidx, "p")
    in_maps = []
    plan = []
    nxt = 0
    for c in range(n):
        if c < DB:
            pidx = list(range(nxt, nxt + NB_))
            nxt += NB_
            xc = np.concatenate([xs[c]] + [xp[i] for i in pidx], axis=0)
            plan.append(("s", c, pidx))
            m = dict(consts, **ms)
        else:
            pidx = list(range(nxt, nxt + NSUB + NB_))
            nxt += NSUB + NB_
            xc = np.concatenate([xp[i] for i in pidx], axis=0)
            plan.append(("p", None, pidx))
            m = dict(consts, **mp)
        m["x"] = np.ascontiguousarray(xc, dtype=np.float32)
        in_maps.append(m)
    res = run_bass_kernel_spmd(nc, in_maps, core_ids=list(range(n)))
    yp = np.empty_like(xp, dtype=np.float32)
    ys = np.empty_like(xs, dtype=np.float32)
    for c in range(n):
        y = res.results[c]["y"]
        mode, sidx, pidx = plan[c]
        off = 0
        if mode == "s":
            ys[sidx] = y[0:DS]
            off = DS
        for i in pidx:
            yp[i] = y[off:off + S]
            off += S
    return (yp, ys)
```
